# Optimizing a Trainium2 kernel written in Bass

```python
import numpy as np
import jax
import jax.numpy as jnp
from jax import lax

D_MODEL = 1024
BATCH = 32
SEQ = 256
DEPTH = 2
DEC_BATCH = 2
DEC_SEQ = 2048
PAST_LEN = 512

GRID_W = 64
HEAD_DIM = 64
MIX_WIDTH = D_MODEL
A_HEADS = 4
A_DK = 64
A_DV = 64
A_WIDTH = A_HEADS * A_DV
B_WIDTH = D_MODEL // 4
CONV_K = 31
CONV_PAD = (CONV_K - 1) // 2
C_HEADS = 8
C_KV_HEADS = 2
C_WIDTH = C_HEADS * HEAD_DIM
D_FF = 2816
HGRN_CHUNK = 16
Q_BLOCK = 128
ROPE_THETA = 10000.0
ROPE_PAIRS = HEAD_DIM // 4
N_MOD = 9
ALPHA = (2 * DEPTH) ** 0.25
BETA = (8 * DEPTH) ** -0.25
F_MIN = 1e-6
IN_SIZES = (A_WIDTH, A_WIDTH, A_WIDTH, A_WIDTH, A_WIDTH, 2 * B_WIDTH, C_WIDTH,
            C_KV_HEADS * HEAD_DIM, C_KV_HEADS * HEAD_DIM)
IN_WIDTH = 5 * A_WIDTH + 2 * B_WIDTH + C_WIDTH + 2 * C_KV_HEADS * HEAD_DIM

kernel_name = "hybrid_hgrn2_conformer_gqa_diffusion_step"


def _layer_norm(x, g, b, eps=1e-5):
    xf = x.astype(jnp.float32)
    mu = jnp.mean(xf, axis=-1, keepdims=True)
    var = jnp.mean(jnp.square(xf - mu), axis=-1, keepdims=True)
    return ((xf - mu) * lax.rsqrt(var + eps) * g + b).astype(x.dtype)


def _rms_norm(x, g, eps=1e-6):
    xf = x.astype(jnp.float32)
    return (xf * lax.rsqrt(jnp.mean(jnp.square(xf), axis=-1, keepdims=True) + eps) * g).astype(x.dtype)


def _modulate(x, shift, scale):
    return x * (1.0 + scale) + shift


def _swiglu(h, w_in, w_out):
    gate, up = jnp.split(h @ w_in, 2, axis=-1)
    return (jax.nn.silu(gate) * up) @ w_out


def _axial_rope_tables(seq):
    rows = seq // GRID_W
    row_id = jnp.repeat(jnp.arange(rows), GRID_W).astype(jnp.float32)
    col_id = jnp.tile(jnp.arange(GRID_W), rows).astype(jnp.float32)
    inv = ROPE_THETA ** (-jnp.arange(ROPE_PAIRS, dtype=jnp.float32) / ROPE_PAIRS)
    ang = jnp.stack([row_id[:, None] * inv, col_id[:, None] * inv], axis=0)
    return jnp.cos(ang), jnp.sin(ang)


def _apply_rope(x, cos, sin):
    halves = jnp.split(x.astype(jnp.float32), 2, axis=-1)
    out = []
    for axis in range(2):
        x1, x2 = jnp.split(halves[axis], 2, axis=-1)
        cs = cos[axis][None, :, None, :]
        sn = sin[axis][None, :, None, :]
        out += [x1 * cs - x2 * sn, x2 * cs + x1 * sn]
    return jnp.concatenate(out, axis=-1).astype(x.dtype)


def _block_attention(q, k, v):
    bsz, seq_q, heads, hd = q.shape
    kvh = k.shape[2]
    groups = heads // kvh
    nblk = seq_q // Q_BLOCK
    qb = jnp.moveaxis(q.reshape(bsz, nblk, Q_BLOCK, kvh, groups, hd), 1, 0)
    scale = hd ** -0.5

    def one_block(q_blk):
        s = jnp.einsum("bqkgd,bskd->bkgqs", q_blk, k).astype(jnp.float32) * scale
        p = jax.nn.softmax(s, axis=-1).astype(v.dtype)
        return jnp.einsum("bkgqs,bskd->bqkgd", p, v)

    o = lax.map(one_block, qb)
    return jnp.moveaxis(o, 0, 1).reshape(bsz, seq_q, heads * hd)


def _hgrn_scan(q, k, v, logf, s0):
    bsz, seq, heads, _ = q.shape
    dv = v.shape[-1]
    n = seq // HGRN_CHUNK
    chunk = lambda t: t.astype(jnp.float32).reshape(bsz, n, HGRN_CHUNK, heads, t.shape[-1])
    q, k, v, logf = chunk(q), chunk(k), chunk(v), chunk(logf)
    a = jnp.cumsum(logf, axis=2)
    a_last = a[:, :, -1]
    tri = jnp.tril(jnp.ones((HGRN_CHUNK, HGRN_CHUNK), dtype=bool))[None, None, :, :, None, None]
    diff = a[:, :, :, None] - a[:, :, None, :]
    decay = jnp.where(tri, jnp.exp(jnp.where(tri, diff, 0.0)), 0.0)
    scores = jnp.einsum("bntshd,bnshd->bnhts", q[:, :, :, None] * decay, k)
    o_intra = jnp.einsum("bnhts,bnshe->bnthe", scores, v)
    kv = jnp.einsum("bnshd,bnshe->bnhde", k * jnp.exp(a_last[:, :, None] - a), v)

    def step(state, inp):
        dec, kv_n = inp
        return dec[..., None] * state + kv_n, state

    s_fin, s_prev = lax.scan(step, s0.astype(jnp.float32),
                             (jnp.moveaxis(jnp.exp(a_last), 1, 0), jnp.moveaxis(kv, 1, 0)))
    s_prev = jnp.moveaxis(s_prev, 0, 1)
    o_inter = jnp.einsum("bnthd,bnhde->bnthe", q * jnp.exp(a), s_prev)
    return (o_intra + o_inter).reshape(bsz, seq, heads, dv), s_fin


def _depthwise_conv(u, w, b):
    out = lax.conv_general_dilated(
        u, w[:, None, :].astype(u.dtype), window_strides=(1,),
        padding=[(CONV_PAD, CONV_PAD)], dimension_numbers=("NWC", "WIO", "NWC"),
        feature_group_count=u.shape[-1])
    return out + b


def _mixer(h, p, lb, ctx):
    bsz, seq, _ = h.shape
    z = h @ p["w_in"]
    cuts = [int(i) for i in np.cumsum(IN_SIZES)[:-1]]
    zq, zi, zff, zfb, zg, zglu, cq, ck, cv = jnp.split(z, cuts, axis=-1)

    heads_a = lambda t: t.reshape(bsz, seq, A_HEADS, -1)
    lb_h = lb.reshape(A_HEADS, A_DK)

    def forget(zf):
        f = lb_h + (1.0 - lb_h) * jax.nn.sigmoid(heads_a(zf).astype(jnp.float32))
        return 1.0 - f, jnp.log(jnp.maximum(f, F_MIN))

    qa, ia = heads_a(zq), heads_a(zi)
    k_fw, logf_fw = forget(zff)
    k_bw, logf_bw = forget(zfb)
    s0 = jnp.zeros((bsz, 2, A_HEADS, A_DK, A_DV), jnp.float32) if ctx is None else ctx[2]
    rev = lambda t: t[:, ::-1]
    o_fw, s_fw = _hgrn_scan(qa, k_fw, ia, logf_fw, s0[:, 0])
    o_bw, s_bw = _hgrn_scan(rev(qa), rev(k_bw), rev(ia), rev(logf_bw), s0[:, 1])
    o_a = _rms_norm(o_fw + rev(o_bw), p["hgrn_norm_g"].reshape(A_HEADS, A_DV))
    o_a = (o_a * jax.nn.silu(heads_a(zg).astype(jnp.float32))).astype(h.dtype).reshape(bsz, seq, A_WIDTH)

    glu_a, glu_b = jnp.split(zglu, 2, axis=-1)
    u = _depthwise_conv(glu_a * jax.nn.sigmoid(glu_b), p["conv_w"], p["conv_b"])
    o_conv = jax.nn.silu(_layer_norm(u, p["conv_ln_g"], p["conv_ln_b"]))

    qc = _rms_norm(cq.reshape(bsz, seq, C_HEADS, HEAD_DIM), p["q_norm_g"])
    kc = _rms_norm(ck.reshape(bsz, seq, C_KV_HEADS, HEAD_DIM), p["k_norm_g"])
    vc = cv.reshape(bsz, seq, C_KV_HEADS, HEAD_DIM)
    if ctx is None:
        o_c = _block_attention(qc, kc, vc)
        new = (kc, vc, jnp.stack([s_fw, s_bw], axis=1))
    else:
        cos, sin = _axial_rope_tables(seq)
        k_all = jnp.concatenate([ctx[0].astype(kc.dtype), _apply_rope(kc, cos, sin)], axis=1)
        v_all = jnp.concatenate([ctx[1].astype(vc.dtype), vc], axis=1)
        o_c = _block_attention(_apply_rope(qc, cos, sin), k_all, v_all)
        new = None

    y = jnp.concatenate([o_a, o_conv, o_c], axis=-1) @ p["w_out"]
    return y, new


def _layer(x, cond, p, lb, ctx):
    mod = (jax.nn.silu(cond) @ p["w_mod"] + p["b_mod"])[:, None, :]
    sh1, sc1, g1, sh2, sc2, g2, sh3, sc3, g3 = jnp.split(mod, N_MOD, axis=-1)
    f1 = _swiglu(_modulate(x, sh1, sc1), p["ffn1_w_in"], p["ffn1_w_out"])
    x = _layer_norm(ALPHA * x + 0.5 * g1 * f1, p["ln_g"][0], p["ln_b"][0])
    y, new = _mixer(_modulate(x, sh2, sc2), p, lb, ctx)
    x = _layer_norm(ALPHA * x + g2 * y, p["ln_g"][1], p["ln_b"][1])
    f2 = _swiglu(_modulate(x, sh3, sc3), p["ffn2_w_in"], p["ffn2_w_out"])
    x = _layer_norm(ALPHA * x + 0.5 * g3 * f2, p["ln_g"][2], p["ln_b"][2])
    return x, new


def setup_inputs(seed: int = 0) -> dict:
    key = jax.random.key(seed)
    ks = jax.random.split(key, 26)

    def nrm(i, shape, scale):
        return jax.random.normal(ks[i], shape, jnp.float32) * scale

    return {
        "x_prompt": nrm(0, (BATCH, SEQ, D_MODEL), 1.0),
        "x_sample": nrm(1, (DEC_BATCH, DEC_SEQ, D_MODEL), 1.0),
        "cache_k": nrm(2, (DEC_BATCH, DEPTH, PAST_LEN, C_KV_HEADS, HEAD_DIM), 1.0),
        "cache_v": nrm(3, (DEC_BATCH, DEPTH, PAST_LEN, C_KV_HEADS, HEAD_DIM), 1.0),
        "state_hgrn": nrm(4, (DEC_BATCH, DEPTH, 2, A_HEADS, A_DK, A_DV), 0.5),
        "c": nrm(5, (DEC_BATCH, D_MODEL), 1.0),
        "c_ctx": nrm(6, (D_MODEL,), 1.0),
        "w_mod": nrm(7, (DEPTH, D_MODEL, N_MOD * D_MODEL), 0.5 * D_MODEL ** -0.5),
        "b_mod": nrm(8, (DEPTH, N_MOD * D_MODEL), 0.02),
        "ln_g": jnp.ones((DEPTH, 3, D_MODEL), jnp.float32) + nrm(9, (DEPTH, 3, D_MODEL), 0.02),
        "ln_b": nrm(10, (DEPTH, 3, D_MODEL), 0.02),
        "ffn1_w_in": nrm(11, (DEPTH, D_MODEL, 2 * D_FF), D_MODEL ** -0.5),
        "ffn1_w_out": nrm(12, (DEPTH, D_FF, D_MODEL), BETA * D_FF ** -0.5),
        "ffn2_w_in": nrm(13, (DEPTH, D_MODEL, 2 * D_FF), D_MODEL ** -0.5),
        "ffn2_w_out": nrm(14, (DEPTH, D_FF, D_MODEL), BETA * D_FF ** -0.5),
        "w_in": nrm(15, (DEPTH, D_MODEL, IN_WIDTH), D_MODEL ** -0.5),
        "w_out": nrm(16, (DEPTH, MIX_WIDTH, D_MODEL), BETA * MIX_WIDTH ** -0.5),
        "hgrn_lb_logits": nrm(17, (DEPTH, A_HEADS * A_DK), 1.0),
        "hgrn_norm_g": jnp.ones((DEPTH, A_WIDTH), jnp.float32) + nrm(18, (DEPTH, A_WIDTH), 0.02),
        "conv_w": nrm(19, (DEPTH, CONV_K, B_WIDTH), CONV_K ** -0.5),
        "conv_b": nrm(20, (DEPTH, B_WIDTH), 0.02),
        "conv_ln_g": jnp.ones((DEPTH, B_WIDTH), jnp.float32) + nrm(21, (DEPTH, B_WIDTH), 0.02),
        "conv_ln_b": nrm(22, (DEPTH, B_WIDTH), 0.02),
        "q_norm_g": jnp.ones((DEPTH, HEAD_DIM), jnp.float32) + nrm(23, (DEPTH, HEAD_DIM), 0.02),
        "k_norm_g": jnp.ones((DEPTH, HEAD_DIM), jnp.float32) + nrm(24, (DEPTH, HEAD_DIM), 0.02),
    }


def reference(x_prompt, x_sample, cache_k, cache_v, state_hgrn, c, c_ctx, w_mod, b_mod, ln_g, ln_b,
              ffn1_w_in, ffn1_w_out, ffn2_w_in, ffn2_w_out, w_in, w_out, hgrn_lb_logits,
              hgrn_norm_g, conv_w, conv_b, conv_ln_g, conv_ln_b, q_norm_g, k_norm_g):
    lb_soft = jax.nn.softmax(hgrn_lb_logits.astype(jnp.float32), axis=0)
    lbs = jnp.cumsum(lb_soft, axis=0) - lb_soft[0:1]
    xp, xs = x_prompt, x_sample
    ks, vs, ss = [], [], []
    for l in range(DEPTH):
        p = {
            "w_mod": w_mod[l], "b_mod": b_mod[l], "ln_g": ln_g[l], "ln_b": ln_b[l],
            "ffn1_w_in": ffn1_w_in[l], "ffn1_w_out": ffn1_w_out[l],
            "ffn2_w_in": ffn2_w_in[l], "ffn2_w_out": ffn2_w_out[l],
            "w_in": w_in[l], "w_out": w_out[l], "hgrn_norm_g": hgrn_norm_g[l],
            "conv_w": conv_w[l], "conv_b": conv_b[l], "conv_ln_g": conv_ln_g[l],
            "conv_ln_b": conv_ln_b[l], "q_norm_g": q_norm_g[l], "k_norm_g": k_norm_g[l],
        }
        xp, (k_l, v_l, s_l) = _layer(xp, c_ctx[None, :], p, lbs[l], None)
        ks.append(k_l)
        vs.append(v_l)
        ss.append(s_l)
        xs, _ = _layer(xs, c, p, lbs[l], (cache_k[:, l], cache_v[:, l], state_hgrn[:, l]))
    new_cache_k = jnp.stack(ks, axis=1)
    new_cache_v = jnp.stack(vs, axis=1)
    new_state_hgrn = jnp.stack(ss, axis=1)
    return (xp, xs, new_cache_k, new_cache_v, new_state_hgrn)
```

```python
import numpy as np
from contextlib import ExitStack

import concourse.bass as bass
import concourse.mybir as mybir
from concourse.bass_utils import run_bass_kernel_spmd

F32 = mybir.dt.float32
BF16 = mybir.dt.bfloat16
AF = mybir.ActivationFunctionType
ALU = mybir.AluOpType
AX = mybir.AxisListType


class _Op:
    __slots__ = ("eng", "fn", "deps", "is_dma", "sig", "sem", "val", "idx", "final", "waits", "gidx", "fenced")


class _St:
    __slots__ = ("last_w", "readers")

    def __init__(self):
        self.last_w = None
        self.readers = []


def _key(x):
    if isinstance(x, (str, int)):
        return x
    if isinstance(x, tuple):
        return tuple(_key(y) for y in x)
    return x.name


class Prog:
    ENGS = ("pe", "act", "dve", "pool", "sp")
    NSEM = {"pe": 4, "act": 4, "dve": 4, "pool": 4}
    NDMA = {"sp": 16, "act": 8, "pool": 12}

    def __init__(self, nc):
        self.nc = nc
        self.stack = ExitStack()
        self.scopes = []
        self.streams = {e: [] for e in self.ENGS}
        self.state = {}
        self.seg = 0
        self.uid = 0
        self.nidx = {e: 0 for e in self.ENGS}
        self.kcount = {e: 0 for e in self.NSEM}
        self.dcount = {q: 0 for q in self.NDMA}
        self.sems = {e: [self.stack.enter_context(nc.semaphore(f"s_{e}{i}")) for i in range(n)]
                     for e, n in self.NSEM.items()}
        self.dsems = {q: [self.stack.enter_context(nc.semaphore(f"d_{q}{i}")) for i in range(n)]
                      for q, n in self.NDMA.items()}
        self.n_ops = {e: 0 for e in self.ENGS}

    def sb(self, name, shape, dtype):
        self.uid += 1
        st = self.scopes[-1] if self.scopes else self.stack
        self.__dict__.setdefault("allnames", []).append(f"{name}_{self.uid}")
        return st.enter_context(self.nc.sbuf_tensor(f"{name}_{self.uid}", list(shape), dtype))

    def ps(self, name, shape, dtype):
        return self.stack.enter_context(self.nc.psum_tensor(name, list(shape), dtype))

    def push(self):
        self.scopes.append(ExitStack())

    def pop(self):
        self.fence()
        self.flush()
        self.scopes.pop().close()

    def fence(self):
        lasts = []
        for e in ("pe", "act", "dve", "pool"):
            for o in reversed(self.streams[e]):
                if o.fn is not None and not o.is_dma:
                    lasts.append(o)
                    break
        dmas = [o for q in self.NDMA for o in self.streams[q] if o.is_dma]
        for e in self.ENGS:
            o = self.op(e, None)
            o.deps = [d for d in lasts if not (d.eng == e and e == "pe")] + list(dmas)

    def _st(self, k):
        k = _key(k)
        s = self.state.get(k)
        if s is None:
            s = self.state[k] = _St()
        return s

    def op(self, eng, fn, reads=(), writes=(), is_dma=False, final=False):
        o = _Op()
        o.eng, o.fn, o.is_dma, o.final = eng, fn, is_dma, final
        o.sig = False
        o.fenced = False
        o.sem = o.val = None
        o.idx = self.nidx[eng]
        self.nidx[eng] += 1
        o.gidx = self.seg
        pr = [r for r in reads if isinstance(_key(r), str) and _key(r).startswith("pb")]
        if pr:
            reads = [r for r in reads if not (isinstance(_key(r), str) and _key(r).startswith("pb"))]
            writes = list(writes) + pr
        deps = {}
        for r in reads:
            st = self._st(r)
            if st.last_w is not None:
                deps[id(st.last_w)] = (st.last_w, True)
        for w in writes:
            st = self._st(w)
            if st.last_w is not None and id(st.last_w) not in deps:
                deps[id(st.last_w)] = (st.last_w, False)
            for rd in st.readers:
                if id(rd) not in deps:
                    deps[id(rd)] = (rd, False)
        o.deps = []
        for d, raw in deps.values():
            if d is o or d.gidx != self.seg:
                continue
            if d.is_dma or o.is_dma:
                o.deps.append(d)
            elif d.eng == o.eng:
                if o.eng != "pe":
                    o.deps.append(d)
            else:
                o.deps.append(d)
        for r in reads:
            self._st(r).readers.append(o)
        for w in writes:
            st = self._st(w)
            st.last_w = o
            st.readers = []
        self.streams[eng].append(o)
        return o

    def dma(self, eng, out, in_, reads=(), writes=(), final=False):
        return self.op(eng, lambda e: e.dma_start(out=out, in_=in_), reads=reads, writes=writes,
                       is_dma=True, final=final)

    def flush(self):
        nc = self.nc
        streams = self.streams
        slot_of = {}
        for q in self.NDMA:
            nd = self.NDMA[q]
            lst = [o for o in streams[q] if o.is_dma]
            j0 = self.dcount[q]
            for i, o in enumerate(lst):
                j = j0 + i
                slot_of[id(o)] = ((q, j % nd), j // nd + 1)
                o.sem = self.dsems[q][j % nd]
                o.val = 16 * (j // nd + 1)
                if i >= nd:
                    o.deps.append(lst[i - nd])
            self.dcount[q] = j0 + len(lst)
        for eng in self.ENGS:
            seen = {}
            seen_dma = {}
            for o in streams[eng]:
                best = {}
                bestd = {}
                waits = []
                for d in o.deps:
                    if d.is_dma:
                        sl, v = slot_of[id(d)]
                        b = bestd.get(sl)
                        if b is None or v > b[0]:
                            bestd[sl] = (v, d)
                    else:
                        b = best.get(d.eng)
                        if b is None or d.idx > b.idx:
                            best[d.eng] = d
                for sl, (v, d) in bestd.items():
                    if seen_dma.get(sl, 0) < v:
                        seen_dma[sl] = v
                        waits.append(d)
                for pe, d in best.items():
                    if seen.get(pe, -1) < d.idx:
                        seen[pe] = d.idx
                        waits.append(d)
                o.waits = waits
                for d in waits:
                    d.sig = True
        for eng in self.NSEM:
            n = self.NSEM[eng]
            for o in streams[eng]:
                if o.is_dma or o.fn is None:
                    continue
                if o.sig:
                    k = self.kcount[eng]
                    o.sem = self.sems[eng][k % n]
                    o.val = k // n + 1
                    self.kcount[eng] = k + 1

        def replay(eng_name, e):
            for o in streams[eng_name]:
                for d in o.waits:
                    e.wait_ge(d.sem, d.val)
                if o.fn is None:
                    continue
                ins = o.fn(e)
                if o.is_dma:
                    ins.then_inc(o.sem, 16)
                elif o.sig:
                    ins.then_inc(o.sem, 1)

        with nc.Block() as block:
            @block.tensor
            def _(e):
                replay("pe", e)

            @block.scalar
            def _(e):
                replay("act", e)

            @block.vector
            def _(e):
                replay("dve", e)

            @block.gpsimd
            def _(e):
                replay("pool", e)

            @block.sync
            def _(e):
                replay("sp", e)
        for e in self.ENGS:
            self.n_ops[e] += len(streams[e])
            streams[e] = []
        self.seg += 1

    def finish(self):
        self.fence()
        self.flush()
        self.stack.close()


D = 1024
DFF = 2816
NJ = 22
LP = 256
NSEQ = 4
LS = 2048
PAST = 512
ALPHA = 4 ** 0.25
INW = 2560
EPS_LN = 1e-5
EPS_RMS = 1e-6

C_ID, C_ONE, C_BONE, C_MFW, C_MBW, C_RM, C_CM, C_SEG = 0, 128, 256, 384, 512, 640, 768, 776
NCONST = 776 + 512


def _pv_layout():
    off = {}
    n = 0
    for name, sz in (("bmod", 2 * 72), ("lng", 48), ("lnb", 48), ("lbl", 4), ("hg", 4), ("cw", 2 * 2 * 31),
                     ("cb", 4), ("clg", 4), ("clb", 4), ("qg", 2), ("kg", 2), ("cond", 16)):
        off[name] = n
        n += sz
    return off, n


PV, NPV = _pv_layout()


def build_program():
    nc = bass.Bass("TRN2", target_bir_lowering=False)
    P = Prog(nc)
    _STATE = {}

    def din(name, shape):
        return nc.dram_tensor(name, list(shape), F32, kind="ExternalInput").ap()

    def dout(name, shape):
        return nc.dram_tensor(name, list(shape), F32, kind="ExternalOutput").ap()

    xp_d = din("xp", [NSEQ * LP, D])
    xs_d = din("xs", [LS, D])
    ckT_d = din("ckT", [2, 2, 128, PAST])
    cv_d = din("cv", [2, PAST, 128])
    st0_d = din("st0", [2, 2, 2, 128, 64])
    consts_d = din("consts", [128, NCONST])
    rope_d = din("rope", [128, 2 * LS])
    pv_d = din("pv", [128, NPV])
    wmod_d = din("w_mod", [2, D, 9 * D])
    f1i_d = din("ffn1_w_in", [2, D, 2 * DFF])
    f1o_d = din("ffn1_w_out", [2, DFF, D])
    f2i_d = din("ffn2_w_in", [2, D, 2 * DFF])
    f2o_d = din("ffn2_w_out", [2, DFF, D])
    win_d = din("w_in_ext", [2, D, INW + 256])
    wout_d = din("w_out", [2, D, D])
    yp_d = dout("yp", [NSEQ * LP, D])
    ys_d = dout("ys", [LS, D])
    nk_d = dout("nk", [NSEQ, 2, LP, 128])
    nv_d = dout("nv", [NSEQ, 2, LP, 128])
    nst_d = dout("nst", [NSEQ, 2, 2, 2, 128, 64])
    xd_d = nc.dram_tensor("xd_scratch", [128, 8, LS], F32).ap()
    NJG = (NJ + 2) // 3
    wis_d = nc.dram_tensor("wis_scratch", [2, 2, NJG, 128, 8 * 2 * 384], BF16).ap()
    wos_d = nc.dram_tensor("wos_scratch", [2, 2, 4, 128, NJ * 256], BF16).ap()

    cf = P.sb("cf", [128, NCONST], F32)
    cb = P.sb("cb", [128, NCONST], BF16)
    pv = P.sb("pv", [128, NPV], F32)
    mv = P.sb("mv", [128, 2, 2, 72], F32)
    lbv = P.sb("lbv", [128, 2, 2, 2], F32)
    scT = P.sb("scT", [128, 8, 2], BF16)
    NSLOT = 6
    wsl = [P.sb(f"wsl{i}", [128, 2048], BF16) for i in range(NSLOT)]
    banks = [P.ps(f"pb{i}", [128, 512], F32) for i in range(8)]
    st = {"slot": 0, "ring": 0, "acc": 0}

    def slot():
        s_ = wsl[st["slot"] % NSLOT]
        st["slot"] += 1
        return s_

    def ring():
        b = banks[st["ring"] % 6]
        st["ring"] += 1
        return b

    def accb():
        if st.get("modgen") is not None:
            return banks[6]
        b = banks[6 + st["acc"] % 2]
        st["acc"] += 1
        return b

    def step_mod(drain=False):
        g_ = st.get("modgen")
        while g_ is not None:
            try:
                next(g_)
            except StopIteration:
                st["modgen"] = None
                return
            if not drain:
                return

    def pvs(name, i):
        return pv[:, PV[name] + i: PV[name] + i + 1]

    ident = cf[:, C_ID:C_ID + 128]
    ones_b = cb[:, C_ONE:C_ONE + 128]
    bones_b = cb[:, C_BONE:C_BONE + 128]
    rm_b = cb[:, C_RM:C_RM + 128]

    P.dma("sp", cf[:], consts_d, writes=[cf])
    P.dma("sp", pv[:], pv_d, writes=[pv])
    P.op("dve", lambda e: e.tensor_copy(cb[:], cf[:]), reads=[cf], writes=[cb])
    c0 = PV["cond"]
    P.op("act", lambda e: e.activation(scT[:].rearrange("p k c -> p (k c)"), pv[:, c0:c0 + 16], AF.Silu),
         reads=[pv], writes=[scT])
    l0 = PV["lbl"]
    P.op("pool", lambda e: e.memset(lbv[:], 0.0), writes=[lbv])
    P.op("dve", lambda e: e.tensor_tensor(lbv[:, 1, :, 0], pv[:, l0 + 2:l0 + 4], pv[:, l0:l0 + 2], ALU.subtract),
         reads=[pv, lbv], writes=[lbv])
    P.op("act", lambda e: e.activation(lbv[:, 1, :, 0], lbv[:, 1, :, 0], AF.Sigmoid), reads=[lbv], writes=[lbv])
    P.op("dve", lambda e: e.tensor_scalar(lbv[:, :, :, 1], lbv[:, :, :, 0], -1.0, 1.0, ALU.mult, ALU.add),
         reads=[lbv], writes=[lbv])
    def compute_mod(l):
        for _ in compute_mod_gen(l):
            pass

    def compute_mod_gen(l):
        pb = banks[7]
        for mc in range(36):
            if mc > 0:
                yield
            s_ = slot()
            sv = s_[:, 0:2048].rearrange("p (k n) -> p k n", k=8)
            P.dma("pool", sv, wmod_d[l, :, mc * 256:(mc + 1) * 256].rearrange("(k p) n -> p k n", p=128),
                  writes=[s_])
            for mm in range(2):
                m = mc * 2 + mm
                for k in range(8):
                    P.op("pe", lambda e, pb=pb, sv=sv, mm=mm, m=m, k=k: e.matmul(
                        pb[:, 2 * m:2 * m + 2], sv[:, k, mm * 128:(mm + 1) * 128], scT[:, k, :],
                        start=(k == 0), stop=(k == 7)), reads=[s_, scT], writes=[pb])
        b0 = PV["bmod"] + l * 72
        P.op("dve", lambda e, pb=pb, l=l, b0=b0: e.tensor_tensor(
            mv[:, l, :, :].rearrange("p c m -> p m c"), pb[:, 0:144].rearrange("p (m c) -> p m c", c=2),
            pv[:, b0:b0 + 72].unsqueeze(2).to_broadcast([128, 72, 2]), ALU.add), reads=[pb, pv], writes=[mv])
        for s3 in range(3):
            a0 = (3 * s3 + 1) * 8
            P.op("dve", lambda e, l=l, a0=a0: e.tensor_scalar_add(mv[:, l, :, a0:a0 + 8], mv[:, l, :, a0:a0 + 8], 1.0),
                 reads=[mv], writes=[mv])
            if s3 != 1:
                g0 = (3 * s3 + 2) * 8
                P.op("dve", lambda e, l=l, g0=g0: e.tensor_scalar_mul(mv[:, l, :, g0:g0 + 8], mv[:, l, :, g0:g0 + 8], 0.5),
                     reads=[mv], writes=[mv])

    compute_mod(0)

    def mvs(l, c, v, m):
        return mv[:, l, c, v * 8 + m: v * 8 + m + 1]

    def run_group(gi):
        is_s = gi == 1
        cnd = gi
        NT = LS if is_s else NSEQ * LP
        L = LS if is_s else LP
        nseq = 1 if is_s else NSEQ
        ntile = NT // 512
        x_d = xs_d if is_s else xp_d
        y_d = ys_d if is_s else yp_d
        P.push()
        hT = P.sb("hT", [128, 8, NT], BF16)

        def epi_bufs(final):
            d_ = {"xt": P.sb("xt", [128, 8, 512], F32), "tq": P.sb("tq", [128, 8, 512], F32),
                  "tb": P.sb("tb", [128, 8, 512], BF16), "tsq": P.sb("tsq", [128, 8, 512], BF16),
                  "sm": P.sb("sm", [128, 4, 512], F32)}
            if final:
                d_["yt"] = P.sb("yt", [128, 2, 1024], F32)
            return d_

        def epilogue(l, sub, ti, fps, final, eb):
            ts_ = slice(ti * 512, (ti + 1) * 512)
            xt, tq, tb, tsq, sm = eb["xt"], eb["tq"], eb["tb"], eb["tsq"], eb["sm"]
            P.dma("pool", xt[:], xd_d[:, :, ts_], reads=[("xd", ti)], writes=[(xt, m) for m in range(8)])
            for m in range(8):
                pb = fps(m)
                P.op("act", lambda e, pb=pb, m=m: e.activation(tq[:, m, :], pb[:], AF.Copy, scale=mvs(l, cnd, 3 * sub + 2, m)),
                     reads=[pb, mv], writes=[(tq, m)])
                P.op("dve", lambda e, m=m: e.scalar_tensor_tensor(tq[:, m, :], xt[:, m, :], ALPHA, tq[:, m, :], ALU.mult, ALU.add),
                     reads=[(xt, m), (tq, m)], writes=[(tq, m)])
                P.op("dve", lambda e, m=m: e.tensor_copy(tb[:, m, :], tq[:, m, :]), reads=[(tq, m)], writes=[(tb, m)])
                P.op("act", lambda e, m=m: e.activation(tsq[:, m, :], tq[:, m, :], AF.Square), reads=[(tq, m)], writes=[(tsq, m)])
            p1, p2 = ring(), ring()
            for m in range(8):
                P.op("pe", lambda e, m=m, p1=p1: e.matmul(p1[:], ones_b, tb[:, m, :], start=(m == 0), stop=(m == 7)),
                     reads=[(tb, m), cb], writes=[p1])
            for m in range(8):
                P.op("pe", lambda e, m=m, p2=p2: e.matmul(p2[:], ones_b, tsq[:, m, :], start=(m == 0), stop=(m == 7)),
                     reads=[(tsq, m), cb], writes=[p2])
            mean, msq, var, rstd = sm[:, 0, :], sm[:, 1, :], sm[:, 2, :], sm[:, 3, :]
            P.op("act", lambda e: e.activation(mean, p1[:], AF.Copy, scale=1.0 / D), reads=[p1], writes=[(sm, 0)])
            P.op("dve", lambda e: e.tensor_tensor(msq, mean, mean, ALU.mult), reads=[(sm, 0)], writes=[(sm, 1)])
            P.op("dve", lambda e: e.scalar_tensor_tensor(var, p2[:], 1.0 / D, msq, ALU.mult, ALU.subtract),
                 reads=[p2, (sm, 1)], writes=[(sm, 2)])
            P.op("act", lambda e: e.activation(var, var, AF.Ln, bias=pv_eps(EPS_LN)), reads=[(sm, 2)], writes=[(sm, 2)])
            P.op("act", lambda e: e.activation(rstd, var, AF.Exp, scale=-0.5), reads=[(sm, 2)], writes=[(sm, 3)])
            nsub = (sub + 1) % 3
            nl = l if sub < 2 else l + 1

            def post(half):
              for m in range(half * 4, half * 4 + 4):
                P.op("dve", lambda e, m=m: e.tensor_tensor(tq[:, m, :], tq[:, m, :], mean, ALU.subtract),
                     reads=[(tq, m), (sm, 0)], writes=[(tq, m)])
                P.op("dve", lambda e, m=m: e.tensor_tensor(tq[:, m, :], tq[:, m, :], rstd, ALU.mult),
                     reads=[(tq, m), (sm, 3)], writes=[(tq, m)])
                g_ = pvs("lng", (l * 3 + sub) * 8 + m)
                b_ = pvs("lnb", (l * 3 + sub) * 8 + m)
                P.op("dve", lambda e, m=m, g_=g_, b_=b_: e.tensor_scalar(xt[:, m, :], tq[:, m, :], g_, b_, ALU.mult, ALU.add),
                     reads=[(tq, m), pv], writes=[(xt, m)])
                if nl < 2:
                    P.op("act", lambda e, m=m: e.activation(hT[:, m, ts_], xt[:, m, :], AF.Identity,
                                                            scale=mvs(nl, cnd, 3 * nsub + 1, m), bias=mvs(nl, cnd, 3 * nsub, m)),
                         reads=[(xt, m), mv], writes=[(hT, ti)])
              if half == 0:
                return
              if not final:
                P.dma("pool", xd_d[:, :, ts_], xt[:], reads=[(xt, m) for m in range(8)], writes=[("xd", ti)])
              else:
                yt = eb["yt"]
                for blk in range(4):
                    pa, pb2 = ring(), ring()
                    for m in range(8):
                        pq = pa if m < 4 else pb2
                        P.op("pe", lambda e, m=m, blk=blk, pq=pq: e.transpose(
                            pq[:, (m % 4) * 128:(m % 4 + 1) * 128], xt[:, m, blk * 128:(blk + 1) * 128], ident),
                            reads=[(xt, m), cf], writes=[pq])
                    yb = yt[:, blk % 2, :]
                    P.op("act", lambda e, pa=pa, yb=yb: e.copy(yb[:, 0:512], pa[:]), reads=[pa], writes=[(yt, blk % 2)])
                    P.op("dve", lambda e, pb2=pb2, yb=yb: e.tensor_copy(yb[:, 512:1024], pb2[:]), reads=[pb2], writes=[(yt, blk % 2)])
                    r0 = ti * 512 + blk * 128
                    P.dma("pool", y_d[r0:r0 + 128, :], yb, reads=[(yt, blk % 2)], final=True)
            return post

        eps_t = P.sb("eps_t", [128, 2], F32)
        P.op("pool", lambda e: e.memset(eps_t[:, 0:1], EPS_LN), writes=[eps_t])
        P.op("pool", lambda e: e.memset(eps_t[:, 1:2], EPS_RMS), writes=[eps_t])

        def pv_eps(v):
            return eps_t[:, 0:1] if v == EPS_LN else eps_t[:, 1:2]

        P.push()
        xin = P.sb("xin", [128, 4, 1024], F32)
        xo = P.sb("xo", [128, 8, 512], F32)
        for ti in range(ntile):
            for blk in range(4):
                r0 = ti * 512 + blk * 128
                P.dma("sp", xin[:, blk, :], x_d[r0:r0 + 128, :], writes=[(xin, blk)])
            for m in range(8):
                pb = ring()
                for blk in range(4):
                    P.op("pe", lambda e, pb=pb, m=m, blk=blk: e.transpose(
                        pb[:, blk * 128:(blk + 1) * 128], xin[:, blk, m * 128:(m + 1) * 128], ident),
                        reads=[(xin, blk), cf], writes=[pb])
                P.op("act", lambda e, pb=pb, m=m: e.copy(xo[:, m, :], pb[:]), reads=[pb], writes=[(xo, m)])
                P.op("dve", lambda e, pb=pb, m=m, ti=ti: e.tensor_scalar(
                    hT[:, m, ti * 512:(ti + 1) * 512], pb[:], mvs(0, cnd, 1, m), mvs(0, cnd, 0, m), ALU.mult, ALU.add),
                    reads=[pb, mv], writes=[(hT, ti)])
            P.dma("sp", xd_d[:, :, ti * 512:(ti + 1) * 512], xo[:], reads=[(xo, m) for m in range(8)], writes=[("xd", ti)])
        P.pop()

        def ffn(l, sub, wi_d, wo_d, final):
            P.push()
            hid = P.sb("hid", [128, NJ, 512], BF16)
            sg = [P.sb(f"sg{i}", [128, 512], F32) for i in range(2)]
            eb = epi_bufs(final)
            JG = 3
            wis = [P.sb(f"wis{i}", [128, 8, 2, JG * 128], BF16) for i in range(2)]
            wos = [P.sb(f"wos{i}", [128, NJ, 256], BF16) for i in range(2)]
            cnt = {"i": 0, "o": 0}
            pend = {"post": None}
            for ti in range(ntile):
                ts_ = slice(ti * 512, (ti + 1) * 512)
                for jg in range(0, NJ, JG):
                    if pend["post"] is not None and jg == 2 * JG:
                        pend["post"](0)
                    if pend["post"] is not None and jg == 4 * JG:
                        pend["post"](1)
                        pend["post"] = None
                    nj = min(JG, NJ - jg)
                    wsl_ = wis[cnt["i"] % 2]
                    cnt["i"] += 1
                    gidx_ = jg // JG
                    fidx = 0 if sub == 0 else 1
                    wflat = wsl_[:].rearrange("p k s n -> p (k s n)")
                    if gi == 0 and ti == 0:
                        P.dma("pool", wsl_[:, :, 0, 0:nj * 128], wi_d[l, :, jg * 128:(jg + nj) * 128].rearrange("(k p) n -> p k n", p=128), writes=[wsl_])
                        P.dma("pool", wsl_[:, :, 1, 0:nj * 128], wi_d[l, :, DFF + jg * 128:DFF + (jg + nj) * 128].rearrange("(k p) n -> p k n", p=128), writes=[wsl_])
                        P.dma("sp", wis_d[l, fidx, gidx_], wflat, reads=[wsl_], writes=[("wis", l, fidx, gidx_)])
                    else:
                        P.dma("sp", wflat, wis_d[l, fidx, gidx_], reads=[("wis", l, fidx, gidx_)], writes=[wsl_])
                    for jj in range(nj):
                        j = jg + jj
                        pg, pu = ring(), ring()
                        for k in range(8):
                            P.op("pe", lambda e, pg=pg, wsl_=wsl_, k=k, jj=jj, ts_=ts_: e.matmul(pg[:], wsl_[:, k, 0, jj * 128:(jj + 1) * 128], hT[:, k, ts_], start=(k == 0), stop=(k == 7)),
                                 reads=[wsl_, (hT, ti)], writes=[pg])
                        for k in range(8):
                            P.op("pe", lambda e, pu=pu, wsl_=wsl_, k=k, jj=jj, ts_=ts_: e.matmul(pu[:], wsl_[:, k, 1, jj * 128:(jj + 1) * 128], hT[:, k, ts_], start=(k == 0), stop=(k == 7)),
                                 reads=[wsl_, (hT, ti)], writes=[pu])
                        sgt = sg[j % 2]
                        P.op("act", lambda e, pg=pg, sgt=sgt: e.activation(sgt[:], pg[:], AF.Silu), reads=[pg], writes=[sgt])
                        P.op("dve", lambda e, pu=pu, sgt=sgt, j=j: e.tensor_tensor(hid[:, j, :], sgt[:], pu[:], ALU.mult),
                             reads=[pu, sgt], writes=[(hid, j)])

                wcur = {}

                def fps(m, wcur=wcur, ti=ti):
                    if m % 2 == 0:
                        wo_ = wos[cnt["o"] % 2]
                        cnt["o"] += 1
                        fidx = 0 if sub == 0 else 1
                        woflat = wo_[:].rearrange("p j n -> p (j n)")
                        if gi == 0 and ti == 0:
                            P.dma("pool", wo_[:], wo_d[l, :, m * 128:(m + 2) * 128].rearrange("(j p) n -> p j n", p=128), writes=[wo_])
                            P.dma("sp", wos_d[l, fidx, m // 2], woflat, reads=[wo_], writes=[("wos", l, fidx, m // 2)])
                        else:
                            P.dma("sp", woflat, wos_d[l, fidx, m // 2], reads=[("wos", l, fidx, m // 2)], writes=[wo_])
                        wcur["w"] = wo_
                    wo_ = wcur["w"]
                    mo = (m % 2) * 128
                    pb = ring()
                    for j in range(NJ):
                        P.op("pe", lambda e, pb=pb, wo_=wo_, j=j, mo=mo: e.matmul(pb[:], wo_[:, j, mo:mo + 128], hid[:, j, :], start=(j == 0), stop=(j == NJ - 1)),
                             reads=[wo_, (hid, j)], writes=[pb])
                    return pb
                pend["post"] = epilogue(l, sub, ti, fps, final, eb)
            if pend["post"] is not None:
                pend["post"](0)
                pend["post"](1)
            P.pop()

        def zproj(l, col0, ti, width=128):
            raise NotImplementedError

        def mixer(l):
            P.push()
            ocat = P.sb("ocat", [128, 8, NT], BF16)
            lim = _LIMIT.get("lim", 99)
            mixer_hgrn(l, ocat)
            step_mod(drain=True)
            if lim >= 4:
                mixer_conv(l, ocat)
            if lim >= 5:
                mixer_attn(l, ocat)
            if lim < 6:
                P.pop()
                return

            eb = epi_bufs(False)
            eb2 = dict(eb)
            eb2["xt"] = P.sb("xt2", [128, 8, 512], F32)
            eb2["tq"] = P.sb("tq2", [128, 8, 512], F32)
            eb2["sm"] = P.sb("sm2", [128, 4, 512], F32)
            ebs = [eb, eb2]
            posts = []
            for ti in range(ntile):
                def fps(m, ti=ti):
                    s_ = slot()
                    sv = s_[:, 0:1024].rearrange("p (k n) -> p k n", k=8)
                    P.dma("pool", sv, wout_d[l, :, m * 128:(m + 1) * 128].rearrange("(k p) n -> p k n", p=128), writes=[s_])
                    pb = ring()
                    for k in range(8):
                        P.op("pe", lambda e, pb=pb, sv=sv, k=k: e.matmul(pb[:], sv[:, k, :], ocat[:, k, ti * 512:(ti + 1) * 512],
                                                                         start=(k == 0), stop=(k == 7)),
                             reads=[s_, (ocat, k)], writes=[pb])
                    return pb
                post_ = epilogue(l, 1, ti, fps, False, ebs[ti % 2])
                if posts:
                    posts[-1](0)
                    posts[-1](1)
                posts.append(post_)
            posts[-1](0)
            posts[-1](1)
            P.pop()

        def load_w(l, col0, ncol=128):
            s_ = slot()
            sv = s_[:, 0:8 * ncol].rearrange("p (k n) -> p k n", k=8)
            P.dma("pool", sv, win_d[l, :, col0:col0 + ncol].rearrange("(k p) n -> p k n", p=128), writes=[s_])
            return s_, sv

        def zT(l, s_, sv, c0, n, pb, wcol=0):
            ti = c0 // 512
            for k in range(8):
                P.op("pe", lambda e, k=k: e.matmul(pb[:, 0:n], sv[:, k, wcol:wcol + 128], hT[:, k, c0:c0 + n],
                                                   start=(k == 0), stop=(k == 7)), reads=[s_, (hT, ti)], writes=[pb])

        def ztok(l, s_, sv, c0, pb, ncol=128):
            ti = c0 // 512
            for k in range(8):
                P.op("pe", lambda e, k=k: e.matmul(pb[:, 0:ncol], hT[:, k, c0:c0 + 128], sv[:, k, 0:ncol],
                                                   start=(k == 0), stop=(k == 7)), reads=[s_, (hT, ti)], writes=[pb])

        SUB = 512 if is_s else 256

        def mixer_hgrn(l, ocat):
            nb = L // 128
            for pr in range(2):
                P.push()
                sgb = P.sb("sgb", [128, NT], BF16)
                vtok = P.sb("vtok", [128, NT // 128, 128], BF16)
                vz = P.sb("vz", [128, NT // 128, 2, 128], BF16)
                oacc = P.sb("oacc", [128, NT], F32)
                qt = [P.sb(f"qt{d_}", [128, NT], BF16) for d_ in range(2)]
                kt = [P.sb(f"kt{d_}", [128, NT], BF16) for d_ in range(2)]
                khtok = [P.sb(f"khtok{d_}", [128, NT // 128, 128], BF16) for d_ in range(2)]
                G = [P.sb(f"G{d_}", [128, NT // 16], F32) for d_ in range(2)]
                osq = P.sb("osq", [128, SUB], BF16)
                rs = P.sb("rs", [128, SUB], F32)
                P.push()
                wq = load_w(l, 0 + pr * 128)
                wf = [load_w(l, 512 + pr * 128), load_w(l, 768 + pr * 128)]
                wg = load_w(l, 1024 + pr * 128)
                wi = load_w(l, 256 + pr * 128)
                qf = P.sb("qf", [128, NT], F32)
                kh = [P.sb(f"kh{d_}", [128, NT], F32) for d_ in range(2)]
                tmps = [[P.sb(f"ht{d_}_{i}", [128, SUB], F32) for i in range(5)] for d_ in range(2)]
                P.op("pool", lambda e: e.memset(vz[:], 0.0), writes=[vz])
                for c0 in range(0, NT, SUB):
                    pb = ring()
                    zT(l, wq[0], wq[1], c0, SUB, pb)
                    P.op("act", lambda e, pb=pb, c0=c0: e.copy(qf[:, c0:c0 + SUB], pb[:, 0:SUB]), reads=[pb], writes=[(qf, c0)])
                    pb = ring()
                    zT(l, wg[0], wg[1], c0, SUB, pb)
                    P.op("act", lambda e, pb=pb, c0=c0: e.activation(sgb[:, c0:c0 + SUB], pb[:, 0:SUB], AF.Silu), reads=[pb], writes=[(sgb, c0)])
                for b in range(NT // 128):
                    pb = ring()
                    ztok(l, wi[0], wi[1], b * 128, pb)
                    P.op("act", lambda e, pb=pb, b=b: e.copy(vtok[:, b, :], pb[:, 0:128]), reads=[pb], writes=[(vtok, b)])
                    P.op("dve", lambda e, pb=pb, b=b: e.tensor_copy(vz[:, b, 0, 0:64], pb[:, 0:64]), reads=[pb, vz], writes=[(vz, b)])
                    P.op("dve", lambda e, pb=pb, b=b: e.tensor_copy(vz[:, b, 1, 64:128], pb[:, 64:128]), reads=[pb, vz], writes=[(vz, b)])
                lb_ = lbv[:, l, pr, 0:1]
                om_ = lbv[:, l, pr, 1:2]
                def gate_ops(dr, c0, T):
                    ops = []
                    pb = ring()
                    zT(l, wf[dr][0], wf[dr][1], c0, SUB, pb)
                    nch = SUB // 16
                    c3 = T[3][:].rearrange("p (n c) -> p n c", c=16)
                    Tb = c3[:, :, 15:16].to_broadcast([128, nch, 16])
                    Gs = G[dr][:, c0 // 16:(c0 + SUB) // 16]
                    A_ = lambda eng, fn, r, w: ops.append(lambda: P.op(eng, fn, reads=r, writes=w))
                    A_("act", lambda e: e.activation(T[0][:], pb[:, 0:SUB], AF.Sigmoid), [pb], [T[0]])
                    A_("dve", lambda e: e.tensor_scalar(T[0][:], T[0][:], om_, lb_, ALU.mult, ALU.add), [T[0], lbv], [T[0]])
                    A_("dve", lambda e: e.tensor_scalar(T[1][:], T[0][:], -1.0, 1.0, ALU.mult, ALU.add), [T[0]], [T[1]])
                    A_("dve", lambda e: e.tensor_scalar_max(T[0][:], T[0][:], 1e-6), [T[0]], [T[0]])
                    A_("act", lambda e: e.activation(T[2][:], T[0][:], AF.Ln), [T[0]], [T[2]])
                    A_("dve", lambda e: e.tensor_tensor_scan(T[3][:], cf[:, C_SEG:C_SEG + SUB], T[2][:], 0.0, ALU.mult, ALU.add), [T[2], cf], [T[3]])
                    A_("act", lambda e: e.activation(Gs, c3[:, :, 15], AF.Exp), [T[3]], [(G[dr], c0)])
                    if dr == 1:
                        A_("dve", lambda e: e.tensor_tensor(T[4][:].rearrange("p (n c) -> p n c", c=16), Tb, c3, ALU.subtract), [T[3]], [T[4]])
                        A_("dve", lambda e: e.tensor_tensor(T[4][:], T[4][:], T[2][:], ALU.add), [T[4], T[2]], [T[4]])
                        cumt = T[4]
                        A_("dve", lambda e: e.tensor_tensor(T[2][:], T[3][:], T[2][:], ALU.subtract), [T[3], T[2]], [T[2]])
                    else:
                        cumt = T[3]
                        A_("dve", lambda e: e.tensor_tensor(T[2][:].rearrange("p (n c) -> p n c", c=16), Tb, c3, ALU.subtract), [T[3]], [T[2]])
                    A_("act", lambda e: e.activation(T[2][:], T[2][:], AF.Exp), [T[2]], [T[2]])
                    A_("dve", lambda e: e.tensor_tensor(kh[dr][:, c0:c0 + SUB], T[1][:], T[2][:], ALU.mult), [T[1], T[2]], [(kh[dr], c0)])
                    A_("act", lambda e: e.activation(T[0][:], cumt[:], AF.Exp), [cumt], [T[0]])
                    A_("dve", lambda e: e.tensor_tensor(qt[dr][:, c0:c0 + SUB], qf[:, c0:c0 + SUB], T[0][:], ALU.mult), [T[0], (qf, c0)], [(qt[dr], c0)])
                    A_("act", lambda e: e.activation(T[0][:], cumt[:], AF.Exp, scale=-1.0), [cumt], [T[0]])
                    A_("dve", lambda e: e.tensor_tensor(kt[dr][:, c0:c0 + SUB], T[1][:], T[0][:], ALU.mult), [T[0], T[1]], [(kt[dr], c0)])
                    return ops

                for c0 in range(0, NT, SUB):
                    lists = [gate_ops(dr, c0, tmps[dr]) for dr in range(2)]
                    for i in range(max(len(x) for x in lists)):
                        for lst in lists:
                            if i < len(lst):
                                lst[i]()
                for dr in range(2):
                    for b in range(NT // 128):
                        pbt = ring()
                        P.op("pe", lambda e, b=b, pbt=pbt, dr=dr: e.transpose(pbt[:, 0:128], kh[dr][:, b * 128:(b + 1) * 128], ident),
                             reads=[(kh[dr], (b * 128) // SUB * SUB), cf], writes=[pbt])
                        P.op("act", lambda e, b=b, pbt=pbt, dr=dr: e.copy(khtok[dr][:, b, :], pbt[:, 0:128]), reads=[pbt], writes=[(khtok[dr], b)])
                if is_s:
                    P.pop()
                    P.push()
                SA = [[[P.sb(f"SA{d_}_{q}_{i}", [128, 9, 64], F32) for i in range(2)] for q in range(nseq)] for d_ in range(2)]
                NCH = 2 * nseq
                vexs = [P.sb(f"vex{i}", [128, 2, 8, 64], BF16) for i in range(4)]
                spads = [P.sb(f"spad{i}", [128, 8, 128], BF16) for i in range(4)]
                smks = [P.sb(f"smk{i}", [128, 128], BF16) for i in range(4 * NCH if is_s else 2 * NCH + 2)]
                kvss = [P.sb(f"kvs{i}", [128, 8, 64], F32) for i in range(2 * NCH if is_s else NCH + 1)]
                for sp_ in spads:
                    P.op("pool", lambda e, sp_=sp_: e.memset(sp_[:], 0.0), writes=[sp_])
                rot = {"v": 0, "s": 0, "m": 0, "k": 0}
                ENT = (0, 8)
                S0 = (0, 1)
                for dr in range(2):
                    for sq in range(nseq):
                        if is_s:
                            P.dma("sp", SA[dr][sq][0][:, ENT[dr], :], st0_d[l, dr, pr], writes=[SA[dr][sq][0]])
                        else:
                            P.op("pool", lambda e, t_=SA[dr][sq][0], ent=ENT[dr]: e.memset(t_[:, ent, :], 0.0), writes=[SA[dr][sq][0]])
                visited = set()
                chains = [(dr, sq) for dr in range(2) for sq in range(nseq)]

                def phaseA(idx):
                    out = []
                    for (dr, sq) in chains:
                        step_mod()
                        bi = idx if dr == 0 else nb - 1 - idx
                        mcol = C_MFW if dr == 0 else C_MBW
                        b = sq * nb + bi
                        t0 = b * 128
                        vex = vexs[rot["v"] % len(vexs)]
                        rot["v"] += 1
                        P.op("pool", lambda e, b=b, vex=vex: e.tensor_tensor(
                            vex[:], vtok[:, b, :].rearrange("p (h e) -> p h e", h=2).unsqueeze(2).to_broadcast([128, 2, 8, 64]),
                            cf[:, C_CM:C_CM + 8].unsqueeze(1).unsqueeze(3).to_broadcast([128, 2, 8, 64]), ALU.mult),
                            reads=[(vtok, b), cf], writes=[vex])
                        kvs = kvss[rot["k"] % len(kvss)]
                        rot["k"] += 1
                        smk2 = []
                        for h2 in range(2):
                            hs = slice(64 * h2, 64 * h2 + 64)
                            pkv = ring()
                            P.op("pe", lambda e, pkv=pkv, b=b, h2=h2, vex=vex, dr=dr: e.matmul(pkv[:], khtok[dr][:, b, :], vex[:, h2, :, :].rearrange("p n e -> p (n e)"), start=True, stop=True),
                                 reads=[(khtok[dr], b), vex], writes=[pkv])
                            P.op("act", lambda e, pkv=pkv, kvs=kvs, hs=hs: e.copy(kvs[hs].rearrange("p n e -> p (n e)"), pkv[hs, :]), reads=[pkv], writes=[(kvs, h2)])
                            pss = ring()
                            P.op("pe", lambda e, pss=pss, hs=hs, t0=t0, dr=dr: e.matmul(pss[:, 0:128], kt[dr][hs, t0:t0 + 128], qt[dr][hs, t0:t0 + 128], start=True, stop=True),
                                 reads=[(kt[dr], t0 // SUB * SUB), (qt[dr], t0 // SUB * SUB)], writes=[pss])
                            smk = smks[rot["m"] % len(smks)]
                            rot["m"] += 1
                            P.op("dve", lambda e, pss=pss, smk=smk, mcol=mcol: e.tensor_tensor(smk[:], pss[:, 0:128], cf[:, mcol:mcol + 128], ALU.mult),
                                 reads=[pss, cf], writes=[smk])
                            smk2.append(smk)
                        out.append({"dr": dr, "sq": sq, "b": b, "t0": t0, "kvs": kvs, "smk2": smk2,
                                    "cur": SA[dr][sq][idx % 2], "nxt": SA[dr][sq][(idx + 1) % 2]})
                    return out

                def phaseB(sts):
                    for i in range(8):
                        for c_ in sts:
                            dr, cur, nxt, kvs, t0 = c_["dr"], c_["cur"], c_["nxt"], c_["kvs"], c_["t0"]
                            ent, s0 = ENT[dr], S0[dr]
                            n = i if dr == 0 else 7 - i
                            src_ = cur[:, n + s0, :]
                            if i < 7:
                                dst_, dkey = cur[:, (n + 1 - s0), :], cur
                            else:
                                dst_, dkey = nxt[:, ent, :], nxt
                            gcol = (t0 // 16) + n
                            P.op("dve", lambda e, src_=src_, dst_=dst_, kvs=kvs, n=n, gcol=gcol, dr=dr: e.scalar_tensor_tensor(
                                dst_, src_, G[dr][:, gcol:gcol + 1], kvs[:, n, :], ALU.mult, ALU.add),
                                reads=[cur, (kvs, 0), (kvs, 1), (G[dr], t0 // SUB * SUB)], writes=[dkey])

                def phaseC(sts):
                    for c_ in sts:
                        dr, cur, b, t0, smk2 = c_["dr"], c_["cur"], c_["b"], c_["t0"], c_["smk2"]
                        s0 = S0[dr]
                        spad = spads[rot["s"] % len(spads)]
                        rot["s"] += 1
                        P.op("act", lambda e, spad=spad, cur=cur, s0=s0: e.copy(spad[0:64, :, 0:64], cur[0:64, s0:s0 + 8, :]), reads=[cur], writes=[(spad, 0)])
                        P.op("pool", lambda e, spad=spad, cur=cur, s0=s0: e.tensor_copy(spad[64:128, :, 64:128], cur[64:128, s0:s0 + 8, :]), reads=[cur], writes=[(spad, 1)])
                        po = accb()
                        for h2 in range(2):
                            P.op("pe", lambda e, po=po, b=b, h2=h2, smk=smk2[h2]: e.matmul(po[:, 0:128], vz[:, b, h2, :], smk[:], start=(h2 == 0), stop=False),
                                 reads=[(vz, b), smk2[h2]], writes=[po])
                        for n in range(8):
                            P.op("pe", lambda e, po=po, spad=spad, n=n, t0=t0, dr=dr: e.matmul(
                                po[:, 16 * n:16 * n + 16], spad[:, n, :], qt[dr][:, t0 + 16 * n:t0 + 16 * n + 16], start=False, stop=(n == 7)),
                                reads=[(spad, 0), (spad, 1), spad, (qt[dr], t0 // SUB * SUB)], writes=[po])
                        if b not in visited:
                            visited.add(b)
                            P.op("act", lambda e, po=po, t0=t0: e.copy(oacc[:, t0:t0 + 128], po[:, 0:128]), reads=[po], writes=[(oacc, b)])
                        else:
                            P.op("dve", lambda e, po=po, t0=t0: e.tensor_tensor(oacc[:, t0:t0 + 128], oacc[:, t0:t0 + 128], po[:, 0:128], ALU.add),
                                 reads=[po, (oacc, b)], writes=[(oacc, b)])

                if is_s:
                    nxt_st = phaseA(0)
                    for idx in range(nb):
                        cur_st = nxt_st
                        if idx + 1 < nb:
                            nxt_st = phaseA(idx + 1)
                        phaseB(cur_st)
                        phaseC(cur_st)
                else:
                    for idx in range(nb):
                        cur_st = phaseA(idx)
                        phaseB(cur_st)
                        phaseC(cur_st)
                if not is_s:
                    for dr in range(2):
                        for sq in range(nseq):
                            fin = SA[dr][sq][nb % 2]
                            P.dma("sp", nst_d[sq, l, dr, pr], fin[:, ENT[dr], :], reads=[fin], final=True)
                P.pop()
                for c0 in range(0, NT, SUB):
                    P.op("act", lambda e, c0=c0, osq=osq: e.activation(osq[:], oacc[:, c0:c0 + SUB], AF.Square),
                         reads=[(oacc, b) for b in range(c0 // 128, (c0 + SUB) // 128)], writes=[osq])
                    pb = ring()
                    P.op("pe", lambda e, pb=pb, osq=osq: e.matmul(pb[:, 0:SUB], bones_b, osq[:], start=True, stop=True), reads=[osq, cb], writes=[pb])
                    P.op("act", lambda e, pb=pb, rs=rs: e.activation(rs[:], pb[:, 0:SUB], AF.Ln, scale=1.0 / 64, bias=pv_eps(EPS_RMS)), reads=[pb], writes=[rs])
                    P.op("act", lambda e, rs=rs: e.activation(rs[:], rs[:], AF.Exp, scale=-0.5), reads=[rs], writes=[rs])
                    P.op("dve", lambda e, rs=rs, c0=c0: e.tensor_tensor(rs[:], rs[:], oacc[:, c0:c0 + SUB], ALU.mult),
                         reads=[rs] + [(oacc, b) for b in range(c0 // 128, (c0 + SUB) // 128)], writes=[rs])
                    hg_ = pvs("hg", l * 2 + pr)
                    P.op("dve", lambda e, rs=rs, c0=c0, hg_=hg_, pr=pr: e.scalar_tensor_tensor(ocat[:, pr, c0:c0 + SUB], rs[:], hg_, sgb[:, c0:c0 + SUB], ALU.mult, ALU.mult),
                         reads=[rs, (sgb, c0), pv], writes=[(ocat, pr)])
                P.pop()

        def mixer_conv(l, ocat):
            P.push()
            LPAD = L + 30
            upad = P.sb("upad", [128, 2, nseq, LPAD], BF16)
            ucf = P.sb("ucf", [128, 2, NT], F32)
            dg = P.sb("dg", [128, 2, 31, 128], BF16)
            ctmp = [P.sb(f"ct{i}", [128, SUB], F32) for i in range(4)]
            cbt = [P.sb(f"cbt{i}", [128, SUB], BF16) for i in range(4)]
            P.op("pool", lambda e: e.memset(upad[:], 0.0), writes=[upad])
            for c in range(2):
                wa = load_w(l, 1280 + c * 128)
                wb_ = load_w(l, 1536 + c * 128)
                for j in range(31):
                    cw_ = pvs("cw", (l * 2 + c) * 31 + j)
                    P.op("dve", lambda e, c=c, j=j, cw_=cw_: e.tensor_scalar_mul(dg[:, c, j, :], ident, cw_), reads=[cf, pv], writes=[(dg, c)])
                for sq in range(nseq):
                    for o0 in range(0, L, SUB):
                        c0 = sq * L + o0
                        pa, pb_ = ring(), ring()
                        zT(l, wa[0], wa[1], c0, SUB, pa)
                        zT(l, wb_[0], wb_[1], c0, SUB, pb_)
                        P.op("act", lambda e, pb_=pb_: e.activation(ctmp[0][:], pb_[:, 0:SUB], AF.Sigmoid), reads=[pb_], writes=[ctmp[0]])
                        P.op("dve", lambda e, pa=pa, c=c, sq=sq, o0=o0: e.tensor_tensor(upad[:, c, sq, 15 + o0:15 + o0 + SUB], pa[:, 0:SUB], ctmp[0][:], ALU.mult),
                             reads=[pa, ctmp[0], upad], writes=[(upad, c, sq)])
                for sq in range(nseq):
                    for o0 in range(0, L, SUB):
                        c0 = sq * L + o0
                        pc = ring()
                        for j in range(31):
                            P.op("pe", lambda e, pc=pc, c=c, j=j, sq=sq, o0=o0: e.matmul(pc[:, 0:SUB], dg[:, c, j, :], upad[:, c, sq, o0 + j:o0 + j + SUB],
                                                                                    start=(j == 0), stop=(j == 30)),
                                 reads=[(dg, c), (upad, c, sq), upad], writes=[pc])
                        P.op("act", lambda e, pc=pc, c=c, c0=c0: e.activation(ucf[:, c, c0:c0 + SUB], pc[:, 0:SUB], AF.Identity, bias=pvs("cb", l * 2 + c)),
                             reads=[pc, pv], writes=[(ucf, c, c0)])
            for c0 in range(0, NT, SUB):
                p1, p2 = ring(), ring()
                for c in range(2):
                    P.op("dve", lambda e, c=c, c0=c0: e.tensor_copy(cbt[c][:], ucf[:, c, c0:c0 + SUB]), reads=[(ucf, c, c0)], writes=[cbt[c]])
                    P.op("act", lambda e, c=c, c0=c0: e.activation(cbt[2 + c][:], ucf[:, c, c0:c0 + SUB], AF.Square), reads=[(ucf, c, c0)], writes=[cbt[2 + c]])
                for c in range(2):
                    P.op("pe", lambda e, c=c, p1=p1: e.matmul(p1[:, 0:SUB], ones_b, cbt[c][:], start=(c == 0), stop=(c == 1)), reads=[cbt[c], cb], writes=[p1])
                for c in range(2):
                    P.op("pe", lambda e, c=c, p2=p2: e.matmul(p2[:, 0:SUB], ones_b, cbt[2 + c][:], start=(c == 0), stop=(c == 1)), reads=[cbt[2 + c], cb], writes=[p2])
                mean, msq, var, rstd = ctmp[0], ctmp[1], ctmp[2], ctmp[3]
                P.op("act", lambda e, p1=p1: e.activation(mean[:], p1[:, 0:SUB], AF.Copy, scale=1.0 / 256), reads=[p1], writes=[mean])
                P.op("dve", lambda e: e.tensor_tensor(msq[:], mean[:], mean[:], ALU.mult), reads=[mean], writes=[msq])
                P.op("dve", lambda e, p2=p2: e.scalar_tensor_tensor(var[:], p2[:, 0:SUB], 1.0 / 256, msq[:], ALU.mult, ALU.subtract), reads=[p2, msq], writes=[var])
                P.op("act", lambda e: e.activation(var[:], var[:], AF.Ln, bias=pv_eps(EPS_LN)), reads=[var], writes=[var])
                P.op("act", lambda e: e.activation(rstd[:], var[:], AF.Exp, scale=-0.5), reads=[var], writes=[rstd])
                for c in range(2):
                    P.op("dve", lambda e, c=c, c0=c0: e.tensor_tensor(msq[:], ucf[:, c, c0:c0 + SUB], mean[:], ALU.subtract), reads=[(ucf, c, c0), mean], writes=[msq])
                    P.op("dve", lambda e: e.tensor_tensor(msq[:], msq[:], rstd[:], ALU.mult), reads=[msq, rstd], writes=[msq])
                    P.op("act", lambda e, c=c, c0=c0: e.activation(ocat[:, 2 + c, c0:c0 + SUB], msq[:], AF.Silu, scale=pvs("clg", l * 2 + c), bias=pvs("clb", l * 2 + c)),
                         reads=[msq, pv], writes=[(ocat, 2 + c)])
            P.pop()

        def mixer_attn(l, ocat):
            P.push()
            KOFF = PAST if is_s else 0
            LK = KOFF + NT
            NKB = LK // 128
            qfin = P.sb("qfin", [128, 4, NT], BF16)
            kT = P.sb("kT", [128, 2, 2, LK], BF16)
            vaug = P.sb("vaug", [128, NKB, 2, 2, 128], BF16)
            ats = [[P.sb(f"at{t}_{i}", [128, SUB], F32) for i in range(4)] for t in range(2)]
            abs_ = [[P.sb(f"ab{t}_{i}", [128, SUB], BF16) for i in range(2)] for t in range(2)]
            pTs = [P.sb(f"pT{i}", [128, SUB], BF16) for i in range(3)]
            rcb = P.sb("rcb", [128, SUB], F32)
            vo = [P.sb(f"vo{i}", [128, 128], F32) for i in range(2)]
            if is_s:
                rope = P.sb("rope", [128, 2 * LS], F32)
                P.dma("sp", rope[:], rope_d, writes=[rope])
                cvf = P.sb("cvf", [128, 4, 128], F32)
                P.dma("sp", cvf[:], cv_d[l].rearrange("(b p) c -> p b c", p=128), writes=[cvf])
            P.op("pool", lambda e: e.memset(kT[:], 0.0), writes=[kT])
            if is_s:
                for g in range(2):
                    for hh in range(2):
                        P.dma("pool", kT[64 * hh:64 * hh + 64, g, hh, 0:PAST], ckT_d[l, g, 64 * hh:64 * hh + 64, :], reads=[kT], writes=[(kT, g, "c")])
            P.op("pool", lambda e: e.memset(vaug[:], 1.0), writes=[vaug])
            if is_s:
                for g in range(2):
                    P.op("dve", lambda e, g=g: e.tensor_copy(vaug[:, 0:4, g, 0, 0:64], cvf[:, :, g * 64:(g + 1) * 64]), reads=[cvf, vaug], writes=[(vaug, "c", g)])
                    P.op("dve", lambda e, g=g: e.tensor_copy(vaug[:, 0:4, g, 1, 64:128], cvf[:, :, g * 64:(g + 1) * 64]), reads=[cvf, vaug], writes=[(vaug, "c", g)])

            def normrope(wcol, gname, dst_fn, keep_f32=None):
                s_, sv = load_w(l, wcol)
                gsc = pvs(gname, l)

                def chain(c0, at, ab):
                    ops = []
                    bk = {}
                    dst, dkey = dst_fn(c0)
                    A_ = lambda fn: ops.append(fn)

                    def s0():
                        bk["pz"] = ring()
                        zT(l, s_, sv, c0, SUB, bk["pz"])
                        P.op("act", lambda e: e.copy(at[0][:], bk["pz"][:, 0:SUB]), reads=[bk["pz"]], writes=[at[0]])
                    A_(s0)
                    A_(lambda: P.op("act", lambda e: e.activation(ab[0][:], bk["pz"][:, 0:SUB], AF.Square), reads=[bk["pz"]], writes=[ab[0]]))

                    def s2():
                        bk["pn"] = ring()
                        P.op("pe", lambda e: e.matmul(bk["pn"][:, 0:SUB], bones_b, ab[0][:], start=True, stop=True), reads=[ab[0], cb], writes=[bk["pn"]])
                    A_(s2)
                    A_(lambda: P.op("act", lambda e: e.activation(at[1][:], bk["pn"][:, 0:SUB], AF.Ln, scale=1.0 / 64, bias=pv_eps(EPS_RMS)), reads=[bk["pn"]], writes=[at[1]]))
                    A_(lambda: P.op("act", lambda e: e.activation(at[1][:], at[1][:], AF.Exp, scale=-0.5), reads=[at[1]], writes=[at[1]]))
                    if not is_s:
                        if keep_f32 is not None:
                            A_(lambda: P.op("dve", lambda e: e.scalar_tensor_tensor(at[2][:], at[0][:], gsc, at[1][:], ALU.mult, ALU.mult),
                                            reads=[at[0], at[1], pv], writes=[at[2]]))
                            A_(lambda: keep_f32(c0, at[2]))
                        else:
                            for (r0_, r1_, dd) in dst:
                                A_(lambda dd=dd, r0_=r0_, r1_=r1_: P.op("dve", lambda e: e.scalar_tensor_tensor(dd, at[0][r0_:r1_, :], gsc[r0_:r1_, :], at[1][r0_:r1_, :], ALU.mult, ALU.mult),
                                                                          reads=[at[0], at[1], pv], writes=[dkey]))
                    else:
                        A_(lambda: P.op("dve", lambda e: e.scalar_tensor_tensor(at[2][:], at[0][:], gsc, at[1][:], ALU.mult, ALU.mult),
                                        reads=[at[0], at[1], pv], writes=[at[2]]))
                        A_(lambda: P.op("dve", lambda e: e.tensor_copy(ab[1][:], at[2][:]), reads=[at[2]], writes=[ab[1]]))

                        def s8():
                            bk["pr"] = ring()
                            P.op("pe", lambda e: e.matmul(bk["pr"][:, 0:SUB], rm_b, ab[1][:], start=True, stop=True), reads=[ab[1], cb], writes=[bk["pr"]])
                        A_(s8)
                        A_(lambda: P.op("dve", lambda e: e.tensor_tensor(at[3][:], at[2][:], rope[:, c0:c0 + SUB], ALU.mult), reads=[at[2], rope], writes=[at[3]]))
                        A_(lambda: P.op("dve", lambda e: e.tensor_tensor(at[2][:], bk["pr"][:, 0:SUB], rope[:, LS + c0:LS + c0 + SUB], ALU.mult),
                                        reads=[bk["pr"], rope, at[2]], writes=[at[2]]))
                        for (r0_, r1_, dd) in dst:
                            A_(lambda dd=dd, r0_=r0_, r1_=r1_: P.op("dve", lambda e: e.tensor_tensor(dd, at[3][r0_:r1_, :], at[2][r0_:r1_, :], ALU.add), reads=[at[2], at[3]], writes=[dkey]))
                    return ops

                c0s = list(range(0, NT, SUB))
                for i in range(0, len(c0s), 2):
                    lists = [chain(c0s[i + t], ats[t], abs_[t]) for t in range(2) if i + t < len(c0s)]
                    for k_ in range(max(len(x) for x in lists)):
                        for lst in lists:
                            if k_ < len(lst):
                                lst[k_]()

            for qc in range(4):
                normrope(1792 + qc * 128, "qg", lambda c0, qc=qc: ([(0, 128, qfin[:, qc, c0:c0 + SUB])], (qfin, qc, c0)))
            for g in range(2):
                normrope(INW + g * 128, "kg", lambda c0, g=g: ([(0, 64, kT[0:64, g, 0, KOFF + c0:KOFF + c0 + SUB]),
                                                                (64, 128, kT[64:128, g, 1, KOFF + c0:KOFF + c0 + SUB])], (kT, g, c0)))
            if not is_s:
                def keep(c0, src):
                    for bb in range(SUB // 128):
                        pt = ring()
                        P.op("pe", lambda e, pt=pt, bb=bb, src=src: e.transpose(pt[:, 0:128], src[:, bb * 128:(bb + 1) * 128], ident), reads=[src, cf], writes=[pt])
                        vv = vo[bb % 2]
                        P.op("act", lambda e, pt=pt, vv=vv: e.copy(vv[:], pt[:, 0:128]), reads=[pt], writes=[vv])
                        tok = c0 + bb * 128
                        P.dma("sp", nk_d[tok // LP, l, tok % LP:tok % LP + 128, :], vv[:], reads=[vv], final=True)
                normrope(2304, "kg", lambda c0: (None, None), keep_f32=keep)
            s_, sv = load_w(l, 2432)
            for b in range(NT // 128):
                pvv = ring()
                ztok(l, s_, sv, b * 128, pvv)
                kb = KOFF // 128 + b
                for g in range(2):
                    P.op("act", lambda e, pvv=pvv, kb=kb, g=g: e.copy(vaug[:, kb, g, 0, 0:64], pvv[:, g * 64:(g + 1) * 64]), reads=[pvv, vaug], writes=[(vaug, kb, g)])
                    P.op("dve", lambda e, pvv=pvv, kb=kb, g=g: e.tensor_copy(vaug[:, kb, g, 1, 64:128], pvv[:, g * 64:(g + 1) * 64]), reads=[pvv, vaug], writes=[(vaug, kb, g)])
                if not is_s:
                    vv = vo[b % 2]
                    P.op("act", lambda e, pvv=pvv, vv=vv: e.copy(vv[:], pvv[:, 0:128]), reads=[pvv], writes=[vv])
                    tok = b * 128
                    P.dma("sp", nv_d[tok // LP, l, tok % LP:tok % LP + 128, :], vv[:], reads=[vv], final=True)
            items = []
            for sq in range(nseq):
                kb0 = 0 if is_s else sq * (L // 128)
                nkb = NKB if is_s else L // 128
                for o0 in range(0, L, SUB):
                    c0 = sq * L + o0
                    for h in range(8):
                        for i in range(nkb):
                            items.append((c0, h, kb0 + i, i, nkb))
            DEPTH = 2
            NPT = 5
            pTs2 = pTs + [P.sb(f"pTx{i}", [128, SUB], BF16) for i in range(NPT - len(pTs))]
            pos = {}
            pTof = {}
            for j in range(len(items) + DEPTH):
                if j < len(items):
                    c0, h, kb, i, nkb = items[j]
                    g, qc, h2 = h // 4, h // 2, h % 2
                    if i == 0:
                        pos[(c0, h)] = accb()
                    pss = ring()
                    kdeps = [(kT, g, "c")] if (is_s and kb < 4) else [(kT, g, ((kb * 128 - KOFF) // SUB) * SUB)]
                    P.op("pe", lambda e, pss=pss, h2=h2, g=g, kb=kb, qc=qc, c0=c0: e.matmul(
                        pss[:, 0:SUB], kT[:, g, h2, kb * 128:(kb + 1) * 128], qfin[:, qc, c0:c0 + SUB], start=True, stop=True),
                        reads=kdeps + [kT, (qfin, qc, c0)], writes=[pss])
                    pT = pTs2[j % NPT]
                    pTof[j] = pT
                    P.op("act", lambda e, pss=pss, pT=pT: e.activation(pT[:], pss[:, 0:SUB], AF.Exp, scale=0.125), reads=[pss], writes=[pT])
                jj = j - DEPTH
                if jj >= 0:
                    c0, h, kb, i, nkb = items[jj]
                    g, qc, h2 = h // 4, h // 2, h % 2
                    hs = slice(64 * h2, 64 * h2 + 64)
                    ds = slice(64 * (1 - h2), 64 * (1 - h2) + 64)
                    po = pos[(c0, h)]
                    pT = pTof.pop(jj)
                    vdeps = [(vaug, "c", g), vaug] if (is_s and kb < 4) else [(vaug, kb, g), vaug]
                    P.op("pe", lambda e, po=po, kb=kb, g=g, h2=h2, pT=pT, i=i, nkb=nkb: e.matmul(
                        po[:, 0:SUB], vaug[:, kb, g, h2, :], pT[:], start=(i == 0), stop=(i == nkb - 1)),
                        reads=vdeps + [pT], writes=[po])
                    if i == nkb - 1:
                        P.op("dve", lambda e, po=po, hs=hs, ds=ds: e.reciprocal(rcb[hs, :], po[ds, 0:SUB]), reads=[po], writes=[rcb])
                        P.op("dve", lambda e, po=po, hs=hs, qc=qc, c0=c0: e.tensor_tensor(ocat[hs, 4 + qc, c0:c0 + SUB], po[hs, 0:SUB], rcb[hs, :], ALU.mult),
                             reads=[po, rcb], writes=[(ocat, 4 + qc)])
            P.pop()

        lim = _LIMIT.get("lim", 99)
        if lim >= 2:
            for l in range(2):
                ffn(l, 0, f1i_d, f1o_d, False)
                if l == 0 and not _STATE.get("mod1"):
                    _STATE["mod1"] = True
                    st["modgen"] = compute_mod_gen(1)
                if lim == 2:
                    step_mod(drain=True)
                    break
                mixer(l)
                if lim <= 6:
                    break
                ffn(l, 2, f2i_d, f2o_d, l == 1)
        P.pop()

    if _LIMIT.get("lim", 99) >= 1:
        run_group(_LIMIT.get("g0", 0))
    if _LIMIT.get("lim", 99) >= 8:
        run_group(1)
    P.finish()
    return nc, P


def _consts():
    c = np.zeros((128, NCONST), np.float32)
    i = np.arange(128)
    c[:, C_ID:C_ID + 128] = np.eye(128, dtype=np.float32)
    c[:, C_ONE:C_ONE + 128] = 1.0
    c[:, C_BONE:C_BONE + 128] = (i[:, None] // 64 == i[None, :] // 64)
    same = (i[:, None] // 16 == i[None, :] // 16)
    c[:, C_MFW:C_MFW + 128] = same & (i[:, None] <= i[None, :])
    c[:, C_MBW:C_MBW + 128] = same & (i[:, None] >= i[None, :])
    d = i % 64
    ii = d % 32
    partner = np.where(ii < 16, i + 16, i - 16)
    rm = np.zeros((128, 128), np.float32)
    rm[partner, i] = 1.0
    c[:, C_RM:C_RM + 128] = rm
    c[:, C_CM:C_CM + 8] = (i[:, None] // 16 == np.arange(8)[None, :])
    seg = np.ones(512, np.float32)
    seg[::16] = 0.0
    c[:, C_SEG:C_SEG + 512] = seg[None, :]
    return c


def _rope():
    t = np.arange(LS)
    row = (t // 64).astype(np.float32)
    col = (t % 64).astype(np.float32)
    inv = (10000.0 ** (-np.arange(16, dtype=np.float32) / 16)).astype(np.float32)
    out = np.zeros((128, 2 * LS), np.float32)
    for p in range(128):
        d = p % 64
        pos = row if d < 32 else col
        ii = d % 32
        ang = (pos * inv[ii % 16]).astype(np.float32)
        out[p, :LS] = np.cos(ang)
        out[p, LS:] = -np.sin(ang) if ii < 16 else np.sin(ang)
    return out


_CACHE = {}
_LIMIT = {}


def kernel(x_prompt, x_sample, cache_k, cache_v, state_hgrn, c, c_ctx, w_mod, b_mod, ln_g, ln_b,
           ffn1_w_in, ffn1_w_out, ffn2_w_in, ffn2_w_out, w_in, w_out, hgrn_lb_logits,
           hgrn_norm_g, conv_w, conv_b, conv_ln_g, conv_ln_b, q_norm_g, k_norm_g):
    f = lambda a: np.ascontiguousarray(np.asarray(a, dtype=np.float32))
    x_prompt, x_sample, cache_k, cache_v, state_hgrn = map(f, (x_prompt, x_sample, cache_k, cache_v, state_hgrn))
    w_in = f(w_in)
    kcols = w_in[:, :, 2304:2432]
    w_in_ext = np.concatenate([w_in, kcols[:, :, 0:64], kcols[:, :, 0:64], kcols[:, :, 64:128], kcols[:, :, 64:128]], axis=2)
    consts = _consts()
    rope = _rope()

    def fm(v, nch):
        return np.asarray(v, np.float32).reshape(nch, 128).T

    shared = {
        "consts": consts, "rope": rope, "w_mod": f(w_mod), "ffn1_w_in": f(ffn1_w_in), "ffn1_w_out": f(ffn1_w_out),
        "ffn2_w_in": f(ffn2_w_in), "ffn2_w_out": f(ffn2_w_out), "w_in_ext": np.ascontiguousarray(w_in_ext), "w_out": f(w_out),
    }
    in_maps = []
    for core in range(8):
        b = core // 4
        pvv = np.zeros((128, NPV), np.float32)
        for l in range(2):
            pvv[:, PV["bmod"] + l * 72: PV["bmod"] + (l + 1) * 72] = fm(np.asarray(b_mod)[l], 72)
            for s3 in range(3):
                o = (l * 3 + s3) * 8
                pvv[:, PV["lng"] + o: PV["lng"] + o + 8] = fm(np.asarray(ln_g)[l, s3], 8)
                pvv[:, PV["lnb"] + o: PV["lnb"] + o + 8] = fm(np.asarray(ln_b)[l, s3], 8)
            pvv[:, PV["lbl"] + l * 2: PV["lbl"] + l * 2 + 2] = fm(np.asarray(hgrn_lb_logits)[l], 2)
            pvv[:, PV["hg"] + l * 2: PV["hg"] + l * 2 + 2] = fm(np.asarray(hgrn_norm_g)[l], 2)
            for cc in range(2):
                o = (l * 2 + cc) * 31
                pvv[:, PV["cw"] + o: PV["cw"] + o + 31] = np.asarray(conv_w, np.float32)[l][:, cc * 128:(cc + 1) * 128].T
            pvv[:, PV["cb"] + l * 2: PV["cb"] + l * 2 + 2] = fm(np.asarray(conv_b)[l], 2)
            pvv[:, PV["clg"] + l * 2: PV["clg"] + l * 2 + 2] = fm(np.asarray(conv_ln_g)[l], 2)
            pvv[:, PV["clb"] + l * 2: PV["clb"] + l * 2 + 2] = fm(np.asarray(conv_ln_b)[l], 2)
            pvv[:, PV["qg"] + l] = np.tile(np.asarray(q_norm_g, np.float32)[l], 2)
            pvv[:, PV["kg"] + l] = np.tile(np.asarray(k_norm_g, np.float32)[l], 2)
        cond = np.stack([fm(c_ctx, 8), fm(np.asarray(c)[b], 8)], axis=2)
        pvv[:, PV["cond"]: PV["cond"] + 16] = cond.reshape(128, 16)
        ck = cache_k[b]
        ckT = np.transpose(ck, (0, 2, 3, 1))
        ckT = np.concatenate([ckT, ckT], axis=2)
        m = dict(shared)
        m.update({
            "xp": x_prompt[core * 4:(core + 1) * 4].reshape(NSEQ * LP, D),
            "xs": x_sample[b],
            "ckT": np.ascontiguousarray(ckT),
            "cv": np.ascontiguousarray(cache_v[b].reshape(2, PAST, 128)),
            "st0": np.ascontiguousarray(state_hgrn[b].reshape(2, 2, 2, 128, 64)),
            "pv": pvv,
        })
        in_maps.append(m)
    if _LIMIT.get("prep_only"):
        return in_maps
    if "nc" not in _CACHE:
        _CACHE["nc"] = build_program()[0]
    res = run_bass_kernel_spmd(_CACHE["nc"], in_maps, core_ids=list(range(8)))
    r = res.results
    y_prompt = np.concatenate([r[i]["yp"].reshape(NSEQ, LP, D) for i in range(8)], axis=0)
    y_sample = np.stack([r[0]["ys"], r[4]["ys"]], axis=0)
    nk = np.concatenate([r[i]["nk"].reshape(NSEQ, 2, LP, 2, 64) for i in range(8)], axis=0)
    nv = np.concatenate([r[i]["nv"].reshape(NSEQ, 2, LP, 2, 64) for i in range(8)], axis=0)
    nst = np.concatenate([r[i]["nst"].reshape(NSEQ, 2, 2, 4, 64, 64) for i in range(8)], axis=0)
    return (y_prompt.astype(np.float32), y_sample.astype(np.float32), nk.astype(np.float32),
            nv.astype(np.float32), nst.astype(np.float32))
```

```python
import numpy as np
from contextlib import ExitStack

import concourse.bass as bass
import concourse.mybir as mybir
from concourse.bass_utils import run_bass_kernel_spmd

F32 = mybir.dt.float32
BF16 = mybir.dt.bfloat16
AF = mybir.ActivationFunctionType
ALU = mybir.AluOpType
AX = mybir.AxisListType


class _Op:
    __slots__ = ("eng", "fn", "deps", "is_dma", "sig", "sem", "val", "idx", "final", "waits", "gidx", "fenced")


class _St:
    __slots__ = ("last_w", "readers")

    def __init__(self):
        self.last_w = None
        self.readers = []


def _key(x):
    if isinstance(x, (str, int)):
        return x
    if isinstance(x, tuple):
        return tuple(_key(y) for y in x)
    return x.name


class Prog:
    ENGS = ("pe", "act", "dve", "pool", "sp")
    NSEM = {"pe": 4, "act": 4, "dve": 4, "pool": 4}
    NDMA = {"sp": 16, "act": 8, "pool": 12}

    def __init__(self, nc):
        self.nc = nc
        self.stack = ExitStack()
        self.scopes = []
        self.streams = {e: [] for e in self.ENGS}
        self.state = {}
        self.seg = 0
        self.uid = 0
        self.nidx = {e: 0 for e in self.ENGS}
        self.kcount = {e: 0 for e in self.NSEM}
        self.dcount = {q: 0 for q in self.NDMA}
        self.sems = {e: [self.stack.enter_context(nc.semaphore(f"s_{e}{i}")) for i in range(n)]
                     for e, n in self.NSEM.items()}
        self.dsems = {q: [self.stack.enter_context(nc.semaphore(f"d_{q}{i}")) for i in range(n)]
                      for q, n in self.NDMA.items()}
        self.n_ops = {e: 0 for e in self.ENGS}

    def sb(self, name, shape, dtype):
        self.uid += 1
        st = self.scopes[-1] if self.scopes else self.stack
        self.__dict__.setdefault("allnames", []).append(f"{name}_{self.uid}")
        return st.enter_context(self.nc.sbuf_tensor(f"{name}_{self.uid}", list(shape), dtype))

    def ps(self, name, shape, dtype):
        return self.stack.enter_context(self.nc.psum_tensor(name, list(shape), dtype))

    def push(self):
        self.scopes.append(ExitStack())

    def pop(self):
        self.fence()
        self.flush()
        self.scopes.pop().close()

    def fence(self):
        lasts = []
        for e in ("pe", "act", "dve", "pool"):
            for o in reversed(self.streams[e]):
                if o.fn is not None and not o.is_dma:
                    lasts.append(o)
                    break
        dmas = [o for q in self.NDMA for o in self.streams[q] if o.is_dma]
        for e in self.ENGS:
            o = self.op(e, None)
            o.deps = [d for d in lasts if not (d.eng == e and e == "pe")] + list(dmas)

    def _st(self, k):
        k = _key(k)
        s = self.state.get(k)
        if s is None:
            s = self.state[k] = _St()
        return s

    def op(self, eng, fn, reads=(), writes=(), is_dma=False, final=False):
        o = _Op()
        o.eng, o.fn, o.is_dma, o.final = eng, fn, is_dma, final
        o.sig = False
        o.fenced = False
        o.sem = o.val = None
        o.idx = self.nidx[eng]
        self.nidx[eng] += 1
        o.gidx = self.seg
        pr = [r for r in reads if isinstance(_key(r), str) and _key(r).startswith("pb")]
        if pr:
            reads = [r for r in reads if not (isinstance(_key(r), str) and _key(r).startswith("pb"))]
            writes = list(writes) + pr
        deps = {}
        for r in reads:
            st = self._st(r)
            if st.last_w is not None:
                deps[id(st.last_w)] = (st.last_w, True)
        for w in writes:
            st = self._st(w)
            if st.last_w is not None and id(st.last_w) not in deps:
                deps[id(st.last_w)] = (st.last_w, False)
            for rd in st.readers:
                if id(rd) not in deps:
                    deps[id(rd)] = (rd, False)
        o.deps = []
        for d, raw in deps.values():
            if d is o or d.gidx != self.seg:
                continue
            if d.is_dma or o.is_dma:
                o.deps.append(d)
            elif d.eng == o.eng:
                if o.eng != "pe":
                    o.deps.append(d)
            else:
                o.deps.append(d)
        for r in reads:
            self._st(r).readers.append(o)
        for w in writes:
            st = self._st(w)
            st.last_w = o
            st.readers = []
        self.streams[eng].append(o)
        return o

    def dma(self, eng, out, in_, reads=(), writes=(), final=False):
        return self.op(eng, lambda e: e.dma_start(out=out, in_=in_), reads=reads, writes=writes,
                       is_dma=True, final=final)

    def flush(self):
        nc = self.nc
        streams = self.streams
        slot_of = {}
        for q in self.NDMA:
            nd = self.NDMA[q]
            lst = [o for o in streams[q] if o.is_dma]
            j0 = self.dcount[q]
            for i, o in enumerate(lst):
                j = j0 + i
                slot_of[id(o)] = ((q, j % nd), j // nd + 1)
                o.sem = self.dsems[q][j % nd]
                o.val = 16 * (j // nd + 1)
                if i >= nd:
                    o.deps.append(lst[i - nd])
            self.dcount[q] = j0 + len(lst)
        for eng in self.ENGS:
            seen = {}
            seen_dma = {}
            for o in streams[eng]:
                best = {}
                bestd = {}
                waits = []
                for d in o.deps:
                    if d.is_dma:
                        sl, v = slot_of[id(d)]
                        b = bestd.get(sl)
                        if b is None or v > b[0]:
                            bestd[sl] = (v, d)
                    else:
                        b = best.get(d.eng)
                        if b is None or d.idx > b.idx:
                            best[d.eng] = d
                for sl, (v, d) in bestd.items():
                    if seen_dma.get(sl, 0) < v:
                        seen_dma[sl] = v
                        waits.append(d)
                for pe, d in best.items():
                    if seen.get(pe, -1) < d.idx:
                        seen[pe] = d.idx
                        waits.append(d)
                o.waits = waits
                for d in waits:
                    d.sig = True
        for eng in self.NSEM:
            n = self.NSEM[eng]
            for o in streams[eng]:
                if o.is_dma or o.fn is None:
                    continue
                if o.sig:
                    k = self.kcount[eng]
                    o.sem = self.sems[eng][k % n]
                    o.val = k // n + 1
                    self.kcount[eng] = k + 1

        def replay(eng_name, e):
            for o in streams[eng_name]:
                for d in o.waits:
                    e.wait_ge(d.sem, d.val)
                if o.fn is None:
                    continue
                ins = o.fn(e)
                if o.is_dma:
                    ins.then_inc(o.sem, 16)
                elif o.sig:
                    ins.then_inc(o.sem, 1)

        with nc.Block() as block:
            @block.tensor
            def _(e):
                replay("pe", e)

            @block.scalar
            def _(e):
                replay("act", e)

            @block.vector
            def _(e):
                replay("dve", e)

            @block.gpsimd
            def _(e):
                replay("pool", e)

            @block.sync
            def _(e):
                replay("sp", e)
        for e in self.ENGS:
            self.n_ops[e] += len(streams[e])
            streams[e] = []
        self.seg += 1

    def finish(self):
        self.fence()
        self.flush()
        self.stack.close()


D = 1024
DFF = 2816
NJ = 22
LP = 256
NSEQ = 4
LS = 2048
PAST = 512
ALPHA = 4 ** 0.25
INW = 2560
EPS_LN = 1e-5
EPS_RMS = 1e-6
FJG = 2

C_ID, C_ONE, C_BONE, C_MFW, C_MBW, C_RM, C_CM, C_SEG = 0, 128, 256, 384, 512, 640, 768, 776
NCONST = 776 + 512


def _pv_layout():
    off = {}
    n = 0
    for name, sz in (("bmod", 2 * 72), ("lng", 48), ("lnb", 48), ("lbl", 4), ("hg", 4), ("cw", 2 * 2 * 31),
                     ("cb", 4), ("clg", 4), ("clb", 4), ("qg", 2), ("kg", 2), ("cond", 16)):
        off[name] = n
        n += sz
    return off, n


PV, NPV = _pv_layout()


def build_program():
    nc = bass.Bass("TRN2", target_bir_lowering=False)
    P = Prog(nc)
    _STATE = {}

    def din(name, shape):
        return nc.dram_tensor(name, list(shape), F32, kind="ExternalInput").ap()

    def dout(name, shape):
        return nc.dram_tensor(name, list(shape), F32, kind="ExternalOutput").ap()

    xp_d = din("xp", [NSEQ * LP, D])
    xs_d = din("xs", [LS, D])
    ckT_d = din("ckT", [2, 2, 128, PAST])
    cv_d = din("cv", [2, PAST, 128])
    st0_d = din("st0", [2, 2, 2, 128, 64])
    consts_d = din("consts", [128, NCONST])
    rope_d = din("rope", [128, 2 * LS])
    pv_d = din("pv", [128, NPV])
    wmod_d = din("w_mod", [2, D, 9 * D])
    f1i_d = din("ffn1_w_in", [2, D, 2 * DFF])
    f1o_d = din("ffn1_w_out", [2, DFF, D])
    f2i_d = din("ffn2_w_in", [2, D, 2 * DFF])
    f2o_d = din("ffn2_w_out", [2, DFF, D])
    win_d = din("w_in_ext", [2, D, INW + 256])
    wout_d = din("w_out", [2, D, D])
    yp_d = dout("yp", [NSEQ * LP, D])
    ys_d = dout("ys", [LS, D])
    nk_d = dout("nk", [NSEQ, 2, LP, 128])
    nv_d = dout("nv", [NSEQ, 2, LP, 128])
    nst_d = dout("nst", [NSEQ, 2, 2, 2, 128, 64])
    xd_d = nc.dram_tensor("xd_scratch", [128, 8, LS], F32).ap()
    NJG = (NJ + FJG - 1) // FJG
    wis_d = nc.dram_tensor("wis_scratch", [2, 2, NJG, 128, 8 * 2 * FJG * 128], BF16).ap()
    wos_d = nc.dram_tensor("wos_scratch", [2, 2, 4, 128, NJ * 256], BF16).ap()

    cf = P.sb("cf", [128, NCONST], F32)
    cb = P.sb("cb", [128, NCONST], BF16)
    pv = P.sb("pv", [128, NPV], F32)
    mv = P.sb("mv", [128, 2, 2, 72], F32)
    lbv = P.sb("lbv", [128, 2, 2, 2], F32)
    scT = P.sb("scT", [128, 8, 2], BF16)
    NSLOT = 6
    wsl = [P.sb(f"wsl{i}", [128, 2048], BF16) for i in range(NSLOT)]
    banks = [P.ps(f"pb{i}", [128, 512], F32) for i in range(8)]
    st = {"slot": 0, "ring": 0, "acc": 0}

    def slot():
        s_ = wsl[st["slot"] % NSLOT]
        st["slot"] += 1
        return s_

    def ring():
        b = banks[st["ring"] % 6]
        st["ring"] += 1
        return b

    def accb():
        if st.get("modgen") is not None:
            return banks[6]
        b = banks[6 + st["acc"] % 2]
        st["acc"] += 1
        return b

    def step_mod(drain=False):
        g_ = st.get("modgen")
        while g_ is not None:
            try:
                next(g_)
            except StopIteration:
                st["modgen"] = None
                return
            if not drain:
                return

    def pvs(name, i):
        return pv[:, PV[name] + i: PV[name] + i + 1]

    ident = cf[:, C_ID:C_ID + 128]
    ones_b = cb[:, C_ONE:C_ONE + 128]
    bones_b = cb[:, C_BONE:C_BONE + 128]
    rm_b = cb[:, C_RM:C_RM + 128]

    P.dma("sp", cf[:], consts_d, writes=[cf])
    P.dma("sp", pv[:], pv_d, writes=[pv])
    P.op("dve", lambda e: e.tensor_copy(cb[:], cf[:]), reads=[cf], writes=[cb])
    c0 = PV["cond"]
    P.op("act", lambda e: e.activation(scT[:].rearrange("p k c -> p (k c)"), pv[:, c0:c0 + 16], AF.Silu),
         reads=[pv], writes=[scT])
    l0 = PV["lbl"]
    P.op("pool", lambda e: e.memset(lbv[:], 0.0), writes=[lbv])
    P.op("dve", lambda e: e.tensor_tensor(lbv[:, 1, :, 0], pv[:, l0 + 2:l0 + 4], pv[:, l0:l0 + 2], ALU.subtract),
         reads=[pv, lbv], writes=[lbv])
    P.op("act", lambda e: e.activation(lbv[:, 1, :, 0], lbv[:, 1, :, 0], AF.Sigmoid), reads=[lbv], writes=[lbv])
    P.op("dve", lambda e: e.tensor_scalar(lbv[:, :, :, 1], lbv[:, :, :, 0], -1.0, 1.0, ALU.mult, ALU.add),
         reads=[lbv], writes=[lbv])
    def compute_mod(l):
        for _ in compute_mod_gen(l):
            pass

    def compute_mod_gen(l, split=False):
        pb = banks[7]
        b0 = PV["bmod"] + l * 72

        def evac(c_lo, c_hi):
            P.op("dve", lambda e: e.tensor_tensor(
                mv[:, l, :, c_lo:c_hi].rearrange("p c m -> p m c"), pb[:, 2 * c_lo:2 * c_hi].rearrange("p (m c) -> p m c", c=2),
                pv[:, b0 + c_lo:b0 + c_hi].unsqueeze(2).to_broadcast([128, c_hi - c_lo, 2]), ALU.add), reads=[pb, pv], writes=[mv])

        def fix_add(a0):
            P.op("dve", lambda e: e.tensor_scalar_add(mv[:, l, :, a0:a0 + 8], mv[:, l, :, a0:a0 + 8], 1.0), reads=[mv], writes=[mv])

        def fix_half(g0):
            P.op("dve", lambda e: e.tensor_scalar_mul(mv[:, l, :, g0:g0 + 8], mv[:, l, :, g0:g0 + 8], 0.5), reads=[mv], writes=[mv])

        for mc in range(36):
            if mc > 0:
                yield
            if split and mc == 8:
                evac(0, 16)
                fix_add(8)
                yield "early"
            s_ = slot()
            sv = s_[:, 0:2048].rearrange("p (k n) -> p k n", k=8)
            P.dma("pool", sv, wmod_d[l, :, mc * 256:(mc + 1) * 256].rearrange("(k p) n -> p k n", p=128),
                  writes=[s_])
            for mm in range(2):
                m = mc * 2 + mm
                for k in range(8):
                    P.op("pe", lambda e, pb=pb, sv=sv, mm=mm, m=m, k=k: e.matmul(
                        pb[:, 2 * m:2 * m + 2], sv[:, k, mm * 128:(mm + 1) * 128], scT[:, k, :],
                        start=(k == 0), stop=(k == 7)), reads=[s_, scT], writes=[pb])
        if split:
            evac(16, 72)
        else:
            evac(0, 72)
            fix_add(8)
        fix_half(16)
        fix_add(32)
        fix_add(56)
        fix_half(64)

    _g0 = compute_mod_gen(0, split=True)
    while next(_g0) != "early":
        pass
    st["modgen"] = _g0


    def mvs(l, c, v, m):
        return mv[:, l, c, v * 8 + m: v * 8 + m + 1]

    def run_group(gi):
        is_s = gi == 1
        cnd = gi
        NT = LS if is_s else NSEQ * LP
        L = LS if is_s else LP
        nseq = 1 if is_s else NSEQ
        ntile = NT // 512
        x_d = xs_d if is_s else xp_d
        y_d = ys_d if is_s else yp_d
        P.push()
        hT = P.sb("hT", [128, 8, NT], BF16)

        def epi_bufs(final):
            d_ = {"xt": P.sb("xt", [128, 8, 512], F32), "tq": P.sb("tq", [128, 8, 512], F32),
                  "tb": P.sb("tb", [128, 8, 512], BF16), "tsq": P.sb("tsq", [128, 8, 512], BF16),
                  "sm": P.sb("sm", [128, 4, 512], F32)}
            if final:
                d_["yt"] = P.sb("yt", [128, 2, 1024], F32)
            return d_

        def epilogue(l, sub, ti, fps, final, eb):
            ts_ = slice(ti * 512, (ti + 1) * 512)
            xt, tq, tb, tsq, sm = eb["xt"], eb["tq"], eb["tb"], eb["tsq"], eb["sm"]
            P.dma("pool", xt[:], xd_d[:, :, ts_], reads=[("xd", ti)], writes=[(xt, m) for m in range(8)])
            for m in range(8):
                pb = fps(m)
                P.op("act", lambda e, pb=pb, m=m: e.activation(tq[:, m, :], pb[:], AF.Copy, scale=mvs(l, cnd, 3 * sub + 2, m)),
                     reads=[pb, mv], writes=[(tq, m)])
                P.op("dve", lambda e, m=m: e.scalar_tensor_tensor(tq[:, m, :], xt[:, m, :], ALPHA, tq[:, m, :], ALU.mult, ALU.add),
                     reads=[(xt, m), (tq, m)], writes=[(tq, m)])
                P.op("dve", lambda e, m=m: e.tensor_copy(tb[:, m, :], tq[:, m, :]), reads=[(tq, m)], writes=[(tb, m)])
                P.op("act", lambda e, m=m: e.activation(tsq[:, m, :], tq[:, m, :], AF.Square), reads=[(tq, m)], writes=[(tsq, m)])
            p1, p2 = ring(), ring()
            for m in range(8):
                P.op("pe", lambda e, m=m, p1=p1: e.matmul(p1[:], ones_b, tb[:, m, :], start=(m == 0), stop=(m == 7)),
                     reads=[(tb, m), cb], writes=[p1])
            for m in range(8):
                P.op("pe", lambda e, m=m, p2=p2: e.matmul(p2[:], ones_b, tsq[:, m, :], start=(m == 0), stop=(m == 7)),
                     reads=[(tsq, m), cb], writes=[p2])
            mean, msq, var, rstd = sm[:, 0, :], sm[:, 1, :], sm[:, 2, :], sm[:, 3, :]
            P.op("act", lambda e: e.activation(mean, p1[:], AF.Copy, scale=1.0 / D), reads=[p1], writes=[(sm, 0)])
            P.op("dve", lambda e: e.tensor_tensor(msq, mean, mean, ALU.mult), reads=[(sm, 0)], writes=[(sm, 1)])
            P.op("dve", lambda e: e.scalar_tensor_tensor(var, p2[:], 1.0 / D, msq, ALU.mult, ALU.subtract),
                 reads=[p2, (sm, 1)], writes=[(sm, 2)])
            P.op("act", lambda e: e.activation(var, var, AF.Ln, bias=pv_eps(EPS_LN)), reads=[(sm, 2)], writes=[(sm, 2)])
            P.op("act", lambda e: e.activation(rstd, var, AF.Exp, scale=-0.5), reads=[(sm, 2)], writes=[(sm, 3)])
            nsub = (sub + 1) % 3
            nl = l if sub < 2 else l + 1

            def post(half):
              for m in range(half * 4, half * 4 + 4):
                P.op("dve", lambda e, m=m: e.tensor_tensor(tq[:, m, :], tq[:, m, :], mean, ALU.subtract),
                     reads=[(tq, m), (sm, 0)], writes=[(tq, m)])
                P.op("dve", lambda e, m=m: e.tensor_tensor(tq[:, m, :], tq[:, m, :], rstd, ALU.mult),
                     reads=[(tq, m), (sm, 3)], writes=[(tq, m)])
                g_ = pvs("lng", (l * 3 + sub) * 8 + m)
                b_ = pvs("lnb", (l * 3 + sub) * 8 + m)
                P.op("dve", lambda e, m=m, g_=g_, b_=b_: e.tensor_scalar(xt[:, m, :], tq[:, m, :], g_, b_, ALU.mult, ALU.add),
                     reads=[(tq, m), pv], writes=[(xt, m)])
                if nl < 2:
                    P.op("act", lambda e, m=m: e.activation(hT[:, m, ts_], xt[:, m, :], AF.Identity,
                                                            scale=mvs(nl, cnd, 3 * nsub + 1, m), bias=mvs(nl, cnd, 3 * nsub, m)),
                         reads=[(xt, m), mv], writes=[(hT, ti)])
              if half == 0:
                return
              if not final:
                P.dma("pool", xd_d[:, :, ts_], xt[:], reads=[(xt, m) for m in range(8)], writes=[("xd", ti)])
              else:
                yt = eb["yt"]
                for blk in range(4):
                    pa, pb2 = ring(), ring()
                    for m in range(8):
                        pq = pa if m < 4 else pb2
                        P.op("pe", lambda e, m=m, blk=blk, pq=pq: e.transpose(
                            pq[:, (m % 4) * 128:(m % 4 + 1) * 128], xt[:, m, blk * 128:(blk + 1) * 128], ident),
                            reads=[(xt, m), cf], writes=[pq])
                    yb = yt[:, blk % 2, :]
                    P.op("act", lambda e, pa=pa, yb=yb: e.copy(yb[:, 0:512], pa[:]), reads=[pa], writes=[(yt, blk % 2)])
                    P.op("dve", lambda e, pb2=pb2, yb=yb: e.tensor_copy(yb[:, 512:1024], pb2[:]), reads=[pb2], writes=[(yt, blk % 2)])
                    r0 = ti * 512 + blk * 128
                    P.dma("pool", y_d[r0:r0 + 128, :], yb, reads=[(yt, blk % 2)], final=True)
            return post

        eps_t = P.sb("eps_t", [128, 2], F32)
        P.op("pool", lambda e: e.memset(eps_t[:, 0:1], EPS_LN), writes=[eps_t])
        P.op("pool", lambda e: e.memset(eps_t[:, 1:2], EPS_RMS), writes=[eps_t])

        def pv_eps(v):
            return eps_t[:, 0:1] if v == EPS_LN else eps_t[:, 1:2]

        P.push()
        xin = P.sb("xin", [128, 4, 1024], F32)
        xo = P.sb("xo", [128, 8, 512], F32)
        for ti in range(ntile):
            for blk in range(4):
                r0 = ti * 512 + blk * 128
                P.dma("sp", xin[:, blk, :], x_d[r0:r0 + 128, :], writes=[(xin, blk)])
            for m in range(8):
                pb = ring()
                for blk in range(4):
                    P.op("pe", lambda e, pb=pb, m=m, blk=blk: e.transpose(
                        pb[:, blk * 128:(blk + 1) * 128], xin[:, blk, m * 128:(m + 1) * 128], ident),
                        reads=[(xin, blk), cf], writes=[pb])
                P.op("act", lambda e, pb=pb, m=m: e.copy(xo[:, m, :], pb[:]), reads=[pb], writes=[(xo, m)])
                P.op("dve", lambda e, pb=pb, m=m, ti=ti: e.tensor_scalar(
                    hT[:, m, ti * 512:(ti + 1) * 512], pb[:], mvs(0, cnd, 1, m), mvs(0, cnd, 0, m), ALU.mult, ALU.add),
                    reads=[pb, mv], writes=[(hT, ti)])
            P.dma("sp", xd_d[:, :, ti * 512:(ti + 1) * 512], xo[:], reads=[(xo, m) for m in range(8)], writes=[("xd", ti)])
        P.pop()

        def ffn(l, sub, wi_d, wo_d, final):
            P.push()
            two = (gi == 0)
            hids = [P.sb("hid", [128, NJ, 512], BF16)]
            if two:
                hids.append(P.sb("hid2", [128, NJ, 512], BF16))
            sg = [P.sb(f"sg{i}", [128, 512], F32) for i in range(2)]
            eb = epi_bufs(final)
            JG = FJG
            wis = [P.sb(f"wis{i}", [128, 8, 2, JG * 128], BF16) for i in range(2)]
            wos = [P.sb(f"wos{i}", [128, NJ, 256], BF16) for i in range(2)]
            cnt = {"i": 0, "o": 0}
            pend = {"post": None}
            fidx = 0 if sub == 0 else 1
            tile_groups = [list(range(ntile))] if two else [[t] for t in range(ntile)]
            for tg in tile_groups:
                fresh = (gi == 0 and tg[0] == 0)
                for jg in range(0, NJ, JG):
                    if pend["post"] is not None and jg == 2 * JG:
                        pend["post"](0)
                    if pend["post"] is not None and jg == 4 * JG:
                        pend["post"](1)
                        pend["post"] = None
                    for _ in range(3):
                        step_mod()
                    nj = min(JG, NJ - jg)
                    wsl_ = wis[cnt["i"] % 2]
                    cnt["i"] += 1
                    gidx_ = jg // JG
                    wflat = wsl_[:].rearrange("p k s n -> p (k s n)")
                    if fresh:
                        P.dma("pool", wsl_[:, :, 0, 0:nj * 128], wi_d[l, :, jg * 128:(jg + nj) * 128].rearrange("(k p) n -> p k n", p=128), writes=[wsl_])
                        P.dma("pool", wsl_[:, :, 1, 0:nj * 128], wi_d[l, :, DFF + jg * 128:DFF + (jg + nj) * 128].rearrange("(k p) n -> p k n", p=128), writes=[wsl_])
                        P.dma("sp", wis_d[l, fidx, gidx_], wflat, reads=[wsl_], writes=[("wis", l, fidx, gidx_)])
                    else:
                        P.dma("sp", wflat, wis_d[l, fidx, gidx_], reads=[("wis", l, fidx, gidx_)], writes=[wsl_])
                    for tpos, ti in enumerate(tg):
                        ts_ = slice(ti * 512, (ti + 1) * 512)
                        hidc = hids[tpos]
                        for jj in range(nj):
                            j = jg + jj
                            pg, pu = ring(), ring()
                            for k in range(8):
                                P.op("pe", lambda e, pg=pg, wsl_=wsl_, k=k, jj=jj, ts_=ts_: e.matmul(pg[:], wsl_[:, k, 0, jj * 128:(jj + 1) * 128], hT[:, k, ts_], start=(k == 0), stop=(k == 7)),
                                     reads=[wsl_, (hT, ti)], writes=[pg])
                            for k in range(8):
                                P.op("pe", lambda e, pu=pu, wsl_=wsl_, k=k, jj=jj, ts_=ts_: e.matmul(pu[:], wsl_[:, k, 1, jj * 128:(jj + 1) * 128], hT[:, k, ts_], start=(k == 0), stop=(k == 7)),
                                     reads=[wsl_, (hT, ti)], writes=[pu])
                            sgt = sg[j % 2]
                            P.op("act", lambda e, pg=pg, sgt=sgt: e.activation(sgt[:], pg[:], AF.Silu), reads=[pg], writes=[sgt])
                            P.op("dve", lambda e, pu=pu, sgt=sgt, j=j, hidc=hidc: e.tensor_tensor(hidc[:, j, :], sgt[:], pu[:], ALU.mult),
                                 reads=[pu, sgt], writes=[(hidc, j)])

                step_mod(drain=True)
                for tpos, ti in enumerate(tg):
                    hidc = hids[tpos]
                    wcur = {}

                    def fps(m, wcur=wcur, ti=ti, hidc=hidc):
                        if m % 2 == 0:
                            wo_ = wos[cnt["o"] % 2]
                            cnt["o"] += 1
                            woflat = wo_[:].rearrange("p j n -> p (j n)")
                            if gi == 0 and ti == 0:
                                P.dma("pool", wo_[:], wo_d[l, :, m * 128:(m + 2) * 128].rearrange("(j p) n -> p j n", p=128), writes=[wo_])
                                P.dma("sp", wos_d[l, fidx, m // 2], woflat, reads=[wo_], writes=[("wos", l, fidx, m // 2)])
                            else:
                                P.dma("sp", woflat, wos_d[l, fidx, m // 2], reads=[("wos", l, fidx, m // 2)], writes=[wo_])
                            wcur["w"] = wo_
                        wo_ = wcur["w"]
                        mo = (m % 2) * 128
                        pb = ring()
                        for j in range(NJ):
                            P.op("pe", lambda e, pb=pb, wo_=wo_, j=j, mo=mo: e.matmul(pb[:], wo_[:, j, mo:mo + 128], hidc[:, j, :], start=(j == 0), stop=(j == NJ - 1)),
                                 reads=[wo_, (hidc, j)], writes=[pb])
                        return pb
                    post_ = epilogue(l, sub, ti, fps, final, eb)
                    if two:
                        post_(0)
                        post_(1)
                    else:
                        pend["post"] = post_
            if pend["post"] is not None:
                pend["post"](0)
                pend["post"](1)
            P.pop()

        def zproj(l, col0, ti, width=128):
            raise NotImplementedError

        def mixer(l):
            P.push()
            ocat = P.sb("ocat", [128, 8, NT], BF16)
            lim = _LIMIT.get("lim", 99)
            mixer_hgrn(l, ocat)
            step_mod(drain=True)
            if lim >= 4:
                mixer_conv(l, ocat)
            if lim >= 5:
                mixer_attn(l, ocat)
            if lim < 6:
                P.pop()
                return

            eb = epi_bufs(False)
            eb2 = dict(eb)
            eb2["xt"] = P.sb("xt2", [128, 8, 512], F32)
            eb2["tq"] = P.sb("tq2", [128, 8, 512], F32)
            eb2["sm"] = P.sb("sm2", [128, 4, 512], F32)
            ebs = [eb, eb2]
            posts = []
            for ti in range(ntile):
                def fps(m, ti=ti):
                    s_ = slot()
                    sv = s_[:, 0:1024].rearrange("p (k n) -> p k n", k=8)
                    P.dma("pool", sv, wout_d[l, :, m * 128:(m + 1) * 128].rearrange("(k p) n -> p k n", p=128), writes=[s_])
                    pb = ring()
                    for k in range(8):
                        P.op("pe", lambda e, pb=pb, sv=sv, k=k: e.matmul(pb[:], sv[:, k, :], ocat[:, k, ti * 512:(ti + 1) * 512],
                                                                         start=(k == 0), stop=(k == 7)),
                             reads=[s_, (ocat, k)], writes=[pb])
                    return pb
                post_ = epilogue(l, 1, ti, fps, False, ebs[ti % 2])
                if posts:
                    posts[-1](0)
                    posts[-1](1)
                posts.append(post_)
            posts[-1](0)
            posts[-1](1)
            P.pop()

        def load_w(l, col0, ncol=128):
            s_ = slot()
            sv = s_[:, 0:8 * ncol].rearrange("p (k n) -> p k n", k=8)
            P.dma("pool", sv, win_d[l, :, col0:col0 + ncol].rearrange("(k p) n -> p k n", p=128), writes=[s_])
            return s_, sv

        def zT(l, s_, sv, c0, n, pb, wcol=0):
            ti = c0 // 512
            for k in range(8):
                P.op("pe", lambda e, k=k: e.matmul(pb[:, 0:n], sv[:, k, wcol:wcol + 128], hT[:, k, c0:c0 + n],
                                                   start=(k == 0), stop=(k == 7)), reads=[s_, (hT, ti)], writes=[pb])

        def ztok(l, s_, sv, c0, pb, ncol=128):
            ti = c0 // 512
            for k in range(8):
                P.op("pe", lambda e, k=k: e.matmul(pb[:, 0:ncol], hT[:, k, c0:c0 + 128], sv[:, k, 0:ncol],
                                                   start=(k == 0), stop=(k == 7)), reads=[s_, (hT, ti)], writes=[pb])

        SUB = 512 if is_s else 256

        def mixer_hgrn(l, ocat):
            nb = L // 128
            for pr in range(2):
                P.push()
                sgb = P.sb("sgb", [128, NT], BF16)
                vtok = P.sb("vtok", [128, NT // 128, 128], BF16)
                vz = P.sb("vz", [128, NT // 128, 2, 128], BF16)
                oacc = P.sb("oacc", [128, NT], F32)
                qt = [P.sb(f"qt{d_}", [128, NT], BF16) for d_ in range(2)]
                kt = [P.sb(f"kt{d_}", [128, NT], BF16) for d_ in range(2)]
                khtok = [P.sb(f"khtok{d_}", [128, NT // 128, 128], BF16) for d_ in range(2)]
                G = [P.sb(f"G{d_}", [128, NT // 16], F32) for d_ in range(2)]
                osq = P.sb("osq", [128, SUB], BF16)
                rs = P.sb("rs", [128, SUB], F32)
                P.push()
                wq = load_w(l, 0 + pr * 128)
                wf = [load_w(l, 512 + pr * 128), load_w(l, 768 + pr * 128)]
                wg = load_w(l, 1024 + pr * 128)
                wi = load_w(l, 256 + pr * 128)
                qf = P.sb("qf", [128, NT], F32)
                kh = [P.sb(f"kh{d_}", [128, NT], F32) for d_ in range(2)]
                tmps = [[P.sb(f"ht{d_}_{i}", [128, SUB], F32) for i in range(5)] for d_ in range(2)]
                P.op("pool", lambda e: e.memset(vz[:], 0.0), writes=[vz])
                for c0 in range(0, NT, SUB):
                    pb = ring()
                    zT(l, wq[0], wq[1], c0, SUB, pb)
                    P.op("act", lambda e, pb=pb, c0=c0: e.copy(qf[:, c0:c0 + SUB], pb[:, 0:SUB]), reads=[pb], writes=[(qf, c0)])
                    pb = ring()
                    zT(l, wg[0], wg[1], c0, SUB, pb)
                    P.op("act", lambda e, pb=pb, c0=c0: e.activation(sgb[:, c0:c0 + SUB], pb[:, 0:SUB], AF.Silu), reads=[pb], writes=[(sgb, c0)])
                for b in range(NT // 128):
                    pb = ring()
                    ztok(l, wi[0], wi[1], b * 128, pb)
                    P.op("act", lambda e, pb=pb, b=b: e.copy(vtok[:, b, :], pb[:, 0:128]), reads=[pb], writes=[(vtok, b)])
                    P.op("dve", lambda e, pb=pb, b=b: e.tensor_copy(vz[:, b, 0, 0:64], pb[:, 0:64]), reads=[pb, vz], writes=[(vz, b)])
                    P.op("dve", lambda e, pb=pb, b=b: e.tensor_copy(vz[:, b, 1, 64:128], pb[:, 64:128]), reads=[pb, vz], writes=[(vz, b)])
                lb_ = lbv[:, l, pr, 0:1]
                om_ = lbv[:, l, pr, 1:2]
                def gate_ops(dr, c0, T):
                    ops = []
                    pb = ring()
                    zT(l, wf[dr][0], wf[dr][1], c0, SUB, pb)
                    nch = SUB // 16
                    c3 = T[3][:].rearrange("p (n c) -> p n c", c=16)
                    Tb = c3[:, :, 15:16].to_broadcast([128, nch, 16])
                    Gs = G[dr][:, c0 // 16:(c0 + SUB) // 16]
                    A_ = lambda eng, fn, r, w: ops.append(lambda: P.op(eng, fn, reads=r, writes=w))
                    A_("act", lambda e: e.activation(T[0][:], pb[:, 0:SUB], AF.Sigmoid), [pb], [T[0]])
                    A_("dve", lambda e: e.tensor_scalar(T[0][:], T[0][:], om_, lb_, ALU.mult, ALU.add), [T[0], lbv], [T[0]])
                    A_("dve", lambda e: e.tensor_scalar(T[1][:], T[0][:], -1.0, 1.0, ALU.mult, ALU.add), [T[0]], [T[1]])
                    A_("dve", lambda e: e.tensor_scalar_max(T[0][:], T[0][:], 1e-6), [T[0]], [T[0]])
                    A_("act", lambda e: e.activation(T[2][:], T[0][:], AF.Ln), [T[0]], [T[2]])
                    A_("dve", lambda e: e.tensor_tensor_scan(T[3][:], cf[:, C_SEG:C_SEG + SUB], T[2][:], 0.0, ALU.mult, ALU.add), [T[2], cf], [T[3]])
                    A_("act", lambda e: e.activation(Gs, c3[:, :, 15], AF.Exp), [T[3]], [(G[dr], c0)])
                    if dr == 1:
                        A_("dve", lambda e: e.tensor_tensor(T[4][:].rearrange("p (n c) -> p n c", c=16), Tb, c3, ALU.subtract), [T[3]], [T[4]])
                        A_("dve", lambda e: e.tensor_tensor(T[4][:], T[4][:], T[2][:], ALU.add), [T[4], T[2]], [T[4]])
                        cumt = T[4]
                        A_("dve", lambda e: e.tensor_tensor(T[2][:], T[3][:], T[2][:], ALU.subtract), [T[3], T[2]], [T[2]])
                    else:
                        cumt = T[3]
                        A_("dve", lambda e: e.tensor_tensor(T[2][:].rearrange("p (n c) -> p n c", c=16), Tb, c3, ALU.subtract), [T[3]], [T[2]])
                    A_("act", lambda e: e.activation(T[2][:], T[2][:], AF.Exp), [T[2]], [T[2]])
                    A_("dve", lambda e: e.tensor_tensor(kh[dr][:, c0:c0 + SUB], T[1][:], T[2][:], ALU.mult), [T[1], T[2]], [(kh[dr], c0)])
                    A_("act", lambda e: e.activation(T[0][:], cumt[:], AF.Exp), [cumt], [T[0]])
                    A_("dve", lambda e: e.tensor_tensor(qt[dr][:, c0:c0 + SUB], qf[:, c0:c0 + SUB], T[0][:], ALU.mult), [T[0], (qf, c0)], [(qt[dr], c0)])
                    A_("act", lambda e: e.activation(T[0][:], cumt[:], AF.Exp, scale=-1.0), [cumt], [T[0]])
                    A_("dve", lambda e: e.tensor_tensor(kt[dr][:, c0:c0 + SUB], T[1][:], T[0][:], ALU.mult), [T[0], T[1]], [(kt[dr], c0)])
                    return ops

                for c0 in range(0, NT, SUB):
                    lists = [gate_ops(dr, c0, tmps[dr]) for dr in range(2)]
                    for i in range(max(len(x) for x in lists)):
                        for lst in lists:
                            if i < len(lst):
                                lst[i]()
                for dr in range(2):
                    for b in range(NT // 128):
                        pbt = ring()
                        P.op("pe", lambda e, b=b, pbt=pbt, dr=dr: e.transpose(pbt[:, 0:128], kh[dr][:, b * 128:(b + 1) * 128], ident),
                             reads=[(kh[dr], (b * 128) // SUB * SUB), cf], writes=[pbt])
                        P.op("act", lambda e, b=b, pbt=pbt, dr=dr: e.copy(khtok[dr][:, b, :], pbt[:, 0:128]), reads=[pbt], writes=[(khtok[dr], b)])
                P.pop()
                P.push()
                SA = [[[P.sb(f"SA{d_}_{q}_{i}", [128, 9, 64], F32) for i in range(2)] for q in range(nseq)] for d_ in range(2)]
                NCH = 2 * nseq
                vexs = [P.sb(f"vex{i}", [128, 2, 8, 64], BF16) for i in range(4)]
                spads = [P.sb(f"spad{i}", [128, 8, 128], BF16) for i in range(4)]
                smks = [P.sb(f"smk{i}", [128, 128], BF16) for i in range(4 * NCH if is_s else 2 * NCH + 2)]
                kvss = [P.sb(f"kvs{i}", [128, 8, 64], F32) for i in range(2 * NCH if is_s else NCH + 1)]
                for sp_ in spads:
                    P.op("pool", lambda e, sp_=sp_: e.memset(sp_[:], 0.0), writes=[sp_])
                rot = {"v": 0, "s": 0, "m": 0, "k": 0}
                ENT = (0, 8)
                S0 = (0, 1)
                for dr in range(2):
                    for sq in range(nseq):
                        if is_s:
                            P.dma("sp", SA[dr][sq][0][:, ENT[dr], :], st0_d[l, dr, pr], writes=[SA[dr][sq][0]])
                        else:
                            P.op("pool", lambda e, t_=SA[dr][sq][0], ent=ENT[dr]: e.memset(t_[:, ent, :], 0.0), writes=[SA[dr][sq][0]])
                visited = set()
                chains = [(dr, sq) for dr in range(2) for sq in range(nseq)]

                def phaseA(idx):
                    out = []
                    for (dr, sq) in chains:
                        step_mod()
                        bi = idx if dr == 0 else nb - 1 - idx
                        mcol = C_MFW if dr == 0 else C_MBW
                        b = sq * nb + bi
                        t0 = b * 128
                        vex = vexs[rot["v"] % len(vexs)]
                        rot["v"] += 1
                        P.op("pool", lambda e, b=b, vex=vex: e.tensor_tensor(
                            vex[:], vtok[:, b, :].rearrange("p (h e) -> p h e", h=2).unsqueeze(2).to_broadcast([128, 2, 8, 64]),
                            cf[:, C_CM:C_CM + 8].unsqueeze(1).unsqueeze(3).to_broadcast([128, 2, 8, 64]), ALU.mult),
                            reads=[(vtok, b), cf], writes=[vex])
                        kvs = kvss[rot["k"] % len(kvss)]
                        rot["k"] += 1
                        smk2 = []
                        for h2 in range(2):
                            hs = slice(64 * h2, 64 * h2 + 64)
                            pkv = ring()
                            P.op("pe", lambda e, pkv=pkv, b=b, h2=h2, vex=vex, dr=dr: e.matmul(pkv[:], khtok[dr][:, b, :], vex[:, h2, :, :].rearrange("p n e -> p (n e)"), start=True, stop=True),
                                 reads=[(khtok[dr], b), vex], writes=[pkv])
                            P.op("act", lambda e, pkv=pkv, kvs=kvs, hs=hs: e.copy(kvs[hs].rearrange("p n e -> p (n e)"), pkv[hs, :]), reads=[pkv], writes=[(kvs, h2)])
                            pss = ring()
                            P.op("pe", lambda e, pss=pss, hs=hs, t0=t0, dr=dr: e.matmul(pss[:, 0:128], kt[dr][hs, t0:t0 + 128], qt[dr][hs, t0:t0 + 128], start=True, stop=True),
                                 reads=[(kt[dr], t0 // SUB * SUB), (qt[dr], t0 // SUB * SUB)], writes=[pss])
                            smk = smks[rot["m"] % len(smks)]
                            rot["m"] += 1
                            P.op("dve", lambda e, pss=pss, smk=smk, mcol=mcol: e.tensor_tensor(smk[:], pss[:, 0:128], cf[:, mcol:mcol + 128], ALU.mult),
                                 reads=[pss, cf], writes=[smk])
                            smk2.append(smk)
                        out.append({"dr": dr, "sq": sq, "b": b, "t0": t0, "kvs": kvs, "smk2": smk2,
                                    "cur": SA[dr][sq][idx % 2], "nxt": SA[dr][sq][(idx + 1) % 2]})
                    return out

                def phaseB(sts):
                    for i in range(8):
                        for c_ in sts:
                            dr, cur, nxt, kvs, t0 = c_["dr"], c_["cur"], c_["nxt"], c_["kvs"], c_["t0"]
                            ent, s0 = ENT[dr], S0[dr]
                            n = i if dr == 0 else 7 - i
                            src_ = cur[:, n + s0, :]
                            if i < 7:
                                dst_, dkey = cur[:, (n + 1 - s0), :], cur
                            else:
                                dst_, dkey = nxt[:, ent, :], nxt
                            gcol = (t0 // 16) + n
                            P.op("dve", lambda e, src_=src_, dst_=dst_, kvs=kvs, n=n, gcol=gcol, dr=dr: e.scalar_tensor_tensor(
                                dst_, src_, G[dr][:, gcol:gcol + 1], kvs[:, n, :], ALU.mult, ALU.add),
                                reads=[cur, (kvs, 0), (kvs, 1), (G[dr], t0 // SUB * SUB)], writes=[dkey])

                def phaseC(sts):
                    for c_ in sts:
                        dr, cur, b, t0, smk2 = c_["dr"], c_["cur"], c_["b"], c_["t0"], c_["smk2"]
                        s0 = S0[dr]
                        spad = spads[rot["s"] % len(spads)]
                        rot["s"] += 1
                        P.op("act", lambda e, spad=spad, cur=cur, s0=s0: e.copy(spad[0:64, :, 0:64], cur[0:64, s0:s0 + 8, :]), reads=[cur], writes=[(spad, 0)])
                        P.op("pool", lambda e, spad=spad, cur=cur, s0=s0: e.tensor_copy(spad[64:128, :, 64:128], cur[64:128, s0:s0 + 8, :]), reads=[cur], writes=[(spad, 1)])
                        po = accb()
                        for h2 in range(2):
                            P.op("pe", lambda e, po=po, b=b, h2=h2, smk=smk2[h2]: e.matmul(po[:, 0:128], vz[:, b, h2, :], smk[:], start=(h2 == 0), stop=False),
                                 reads=[(vz, b), smk2[h2]], writes=[po])
                        for n in range(8):
                            P.op("pe", lambda e, po=po, spad=spad, n=n, t0=t0, dr=dr: e.matmul(
                                po[:, 16 * n:16 * n + 16], spad[:, n, :], qt[dr][:, t0 + 16 * n:t0 + 16 * n + 16], start=False, stop=(n == 7)),
                                reads=[(spad, 0), (spad, 1), spad, (qt[dr], t0 // SUB * SUB)], writes=[po])
                        if b not in visited:
                            visited.add(b)
                            P.op("act", lambda e, po=po, t0=t0: e.copy(oacc[:, t0:t0 + 128], po[:, 0:128]), reads=[po], writes=[(oacc, b)])
                        else:
                            P.op("dve", lambda e, po=po, t0=t0: e.tensor_tensor(oacc[:, t0:t0 + 128], oacc[:, t0:t0 + 128], po[:, 0:128], ALU.add),
                                 reads=[po, (oacc, b)], writes=[(oacc, b)])

                if is_s:
                    nxt_st = phaseA(0)
                    for idx in range(nb):
                        cur_st = nxt_st
                        if idx + 1 < nb:
                            nxt_st = phaseA(idx + 1)
                        phaseB(cur_st)
                        phaseC(cur_st)
                else:
                    for idx in range(nb):
                        cur_st = phaseA(idx)
                        phaseB(cur_st)
                        phaseC(cur_st)
                if not is_s:
                    for dr in range(2):
                        for sq in range(nseq):
                            fin = SA[dr][sq][nb % 2]
                            P.dma("sp", nst_d[sq, l, dr, pr], fin[:, ENT[dr], :], reads=[fin], final=True)
                P.pop()
                for c0 in range(0, NT, SUB):
                    P.op("act", lambda e, c0=c0, osq=osq: e.activation(osq[:], oacc[:, c0:c0 + SUB], AF.Square),
                         reads=[(oacc, b) for b in range(c0 // 128, (c0 + SUB) // 128)], writes=[osq])
                    pb = ring()
                    P.op("pe", lambda e, pb=pb, osq=osq: e.matmul(pb[:, 0:SUB], bones_b, osq[:], start=True, stop=True), reads=[osq, cb], writes=[pb])
                    P.op("act", lambda e, pb=pb, rs=rs: e.activation(rs[:], pb[:, 0:SUB], AF.Ln, scale=1.0 / 64, bias=pv_eps(EPS_RMS)), reads=[pb], writes=[rs])
                    P.op("act", lambda e, rs=rs: e.activation(rs[:], rs[:], AF.Exp, scale=-0.5), reads=[rs], writes=[rs])
                    P.op("dve", lambda e, rs=rs, c0=c0: e.tensor_tensor(rs[:], rs[:], oacc[:, c0:c0 + SUB], ALU.mult),
                         reads=[rs] + [(oacc, b) for b in range(c0 // 128, (c0 + SUB) // 128)], writes=[rs])
                    hg_ = pvs("hg", l * 2 + pr)
                    P.op("dve", lambda e, rs=rs, c0=c0, hg_=hg_, pr=pr: e.scalar_tensor_tensor(ocat[:, pr, c0:c0 + SUB], rs[:], hg_, sgb[:, c0:c0 + SUB], ALU.mult, ALU.mult),
                         reads=[rs, (sgb, c0), pv], writes=[(ocat, pr)])
                P.pop()

        def mixer_conv(l, ocat):
            P.push()
            LPAD = L + 30
            upad = P.sb("upad", [128, 2, nseq, LPAD], BF16)
            ucf = P.sb("ucf", [128, 2, NT], F32)
            dg = P.sb("dg", [128, 2, 31, 128], BF16)
            ctmp = [P.sb(f"ct{i}", [128, SUB], F32) for i in range(4)]
            cbt = [P.sb(f"cbt{i}", [128, SUB], BF16) for i in range(4)]
            P.op("pool", lambda e: e.memset(upad[:], 0.0), writes=[upad])
            for c in range(2):
                wa = load_w(l, 1280 + c * 128)
                wb_ = load_w(l, 1536 + c * 128)
                for j in range(31):
                    cw_ = pvs("cw", (l * 2 + c) * 31 + j)
                    P.op("dve", lambda e, c=c, j=j, cw_=cw_: e.tensor_scalar_mul(dg[:, c, j, :], ident, cw_), reads=[cf, pv], writes=[(dg, c)])
                for sq in range(nseq):
                    for o0 in range(0, L, SUB):
                        c0 = sq * L + o0
                        pa, pb_ = ring(), ring()
                        zT(l, wa[0], wa[1], c0, SUB, pa)
                        zT(l, wb_[0], wb_[1], c0, SUB, pb_)
                        P.op("act", lambda e, pb_=pb_: e.activation(ctmp[0][:], pb_[:, 0:SUB], AF.Sigmoid), reads=[pb_], writes=[ctmp[0]])
                        P.op("dve", lambda e, pa=pa, c=c, sq=sq, o0=o0: e.tensor_tensor(upad[:, c, sq, 15 + o0:15 + o0 + SUB], pa[:, 0:SUB], ctmp[0][:], ALU.mult),
                             reads=[pa, ctmp[0], upad], writes=[(upad, c, sq)])
                for sq in range(nseq):
                    for o0 in range(0, L, SUB):
                        c0 = sq * L + o0
                        pc = ring()
                        for j in range(31):
                            P.op("pe", lambda e, pc=pc, c=c, j=j, sq=sq, o0=o0: e.matmul(pc[:, 0:SUB], dg[:, c, j, :], upad[:, c, sq, o0 + j:o0 + j + SUB],
                                                                                    start=(j == 0), stop=(j == 30)),
                                 reads=[(dg, c), (upad, c, sq), upad], writes=[pc])
                        P.op("act", lambda e, pc=pc, c=c, c0=c0: e.activation(ucf[:, c, c0:c0 + SUB], pc[:, 0:SUB], AF.Identity, bias=pvs("cb", l * 2 + c)),
                             reads=[pc, pv], writes=[(ucf, c, c0)])
            for c0 in range(0, NT, SUB):
                p1, p2 = ring(), ring()
                for c in range(2):
                    P.op("dve", lambda e, c=c, c0=c0: e.tensor_copy(cbt[c][:], ucf[:, c, c0:c0 + SUB]), reads=[(ucf, c, c0)], writes=[cbt[c]])
                    P.op("act", lambda e, c=c, c0=c0: e.activation(cbt[2 + c][:], ucf[:, c, c0:c0 + SUB], AF.Square), reads=[(ucf, c, c0)], writes=[cbt[2 + c]])
                for c in range(2):
                    P.op("pe", lambda e, c=c, p1=p1: e.matmul(p1[:, 0:SUB], ones_b, cbt[c][:], start=(c == 0), stop=(c == 1)), reads=[cbt[c], cb], writes=[p1])
                for c in range(2):
                    P.op("pe", lambda e, c=c, p2=p2: e.matmul(p2[:, 0:SUB], ones_b, cbt[2 + c][:], start=(c == 0), stop=(c == 1)), reads=[cbt[2 + c], cb], writes=[p2])
                mean, msq, var, rstd = ctmp[0], ctmp[1], ctmp[2], ctmp[3]
                P.op("act", lambda e, p1=p1: e.activation(mean[:], p1[:, 0:SUB], AF.Copy, scale=1.0 / 256), reads=[p1], writes=[mean])
                P.op("dve", lambda e: e.tensor_tensor(msq[:], mean[:], mean[:], ALU.mult), reads=[mean], writes=[msq])
                P.op("dve", lambda e, p2=p2: e.scalar_tensor_tensor(var[:], p2[:, 0:SUB], 1.0 / 256, msq[:], ALU.mult, ALU.subtract), reads=[p2, msq], writes=[var])
                P.op("act", lambda e: e.activation(var[:], var[:], AF.Ln, bias=pv_eps(EPS_LN)), reads=[var], writes=[var])
                P.op("act", lambda e: e.activation(rstd[:], var[:], AF.Exp, scale=-0.5), reads=[var], writes=[rstd])
                for c in range(2):
                    P.op("dve", lambda e, c=c, c0=c0: e.tensor_tensor(msq[:], ucf[:, c, c0:c0 + SUB], mean[:], ALU.subtract), reads=[(ucf, c, c0), mean], writes=[msq])
                    P.op("dve", lambda e: e.tensor_tensor(msq[:], msq[:], rstd[:], ALU.mult), reads=[msq, rstd], writes=[msq])
                    P.op("act", lambda e, c=c, c0=c0: e.activation(ocat[:, 2 + c, c0:c0 + SUB], msq[:], AF.Silu, scale=pvs("clg", l * 2 + c), bias=pvs("clb", l * 2 + c)),
                         reads=[msq, pv], writes=[(ocat, 2 + c)])
            P.pop()

        def mixer_attn(l, ocat):
            P.push()
            KOFF = PAST if is_s else 0
            LK = KOFF + NT
            NKB = LK // 128
            qfin = P.sb("qfin", [128, 4, NT], BF16)
            kT = P.sb("kT", [128, 2, 2, LK], BF16)
            vaug = P.sb("vaug", [128, NKB, 2, 2, 128], BF16)
            ats = [[P.sb(f"at{t}_{i}", [128, SUB], F32) for i in range(4)] for t in range(2)]
            abs_ = [[P.sb(f"ab{t}_{i}", [128, SUB], BF16) for i in range(2)] for t in range(2)]
            pTs = [P.sb(f"pT{i}", [128, SUB], BF16) for i in range(3)]
            rcb = P.sb("rcb", [128, SUB], F32)
            vo = [P.sb(f"vo{i}", [128, 128], F32) for i in range(2)]
            if is_s:
                rope = P.sb("rope", [128, 2 * LS], F32)
                P.dma("sp", rope[:], rope_d, writes=[rope])
                cvf = P.sb("cvf", [128, 4, 128], F32)
                P.dma("sp", cvf[:], cv_d[l].rearrange("(b p) c -> p b c", p=128), writes=[cvf])
            P.op("pool", lambda e: e.memset(kT[:], 0.0), writes=[kT])
            if is_s:
                for g in range(2):
                    for hh in range(2):
                        P.dma("pool", kT[64 * hh:64 * hh + 64, g, hh, 0:PAST], ckT_d[l, g, 64 * hh:64 * hh + 64, :], reads=[kT], writes=[(kT, g, "c")])
            P.op("pool", lambda e: e.memset(vaug[:], 1.0), writes=[vaug])
            if is_s:
                for g in range(2):
                    P.op("dve", lambda e, g=g: e.tensor_copy(vaug[:, 0:4, g, 0, 0:64], cvf[:, :, g * 64:(g + 1) * 64]), reads=[cvf, vaug], writes=[(vaug, "c", g)])
                    P.op("dve", lambda e, g=g: e.tensor_copy(vaug[:, 0:4, g, 1, 64:128], cvf[:, :, g * 64:(g + 1) * 64]), reads=[cvf, vaug], writes=[(vaug, "c", g)])

            def normrope(wcol, gname, dst_fn, keep_f32=None):
                s_, sv = load_w(l, wcol)
                gsc = pvs(gname, l)

                def chain(c0, at, ab):
                    ops = []
                    bk = {}
                    dst, dkey = dst_fn(c0)
                    A_ = lambda fn: ops.append(fn)

                    def s0():
                        bk["pz"] = ring()
                        zT(l, s_, sv, c0, SUB, bk["pz"])
                        P.op("act", lambda e: e.copy(at[0][:], bk["pz"][:, 0:SUB]), reads=[bk["pz"]], writes=[at[0]])
                    A_(s0)
                    A_(lambda: P.op("act", lambda e: e.activation(ab[0][:], bk["pz"][:, 0:SUB], AF.Square), reads=[bk["pz"]], writes=[ab[0]]))

                    def s2():
                        bk["pn"] = ring()
                        P.op("pe", lambda e: e.matmul(bk["pn"][:, 0:SUB], bones_b, ab[0][:], start=True, stop=True), reads=[ab[0], cb], writes=[bk["pn"]])
                    A_(s2)
                    A_(lambda: P.op("act", lambda e: e.activation(at[1][:], bk["pn"][:, 0:SUB], AF.Ln, scale=1.0 / 64, bias=pv_eps(EPS_RMS)), reads=[bk["pn"]], writes=[at[1]]))
                    A_(lambda: P.op("act", lambda e: e.activation(at[1][:], at[1][:], AF.Exp, scale=-0.5), reads=[at[1]], writes=[at[1]]))
                    if not is_s:
                        if keep_f32 is not None:
                            A_(lambda: P.op("dve", lambda e: e.scalar_tensor_tensor(at[2][:], at[0][:], gsc, at[1][:], ALU.mult, ALU.mult),
                                            reads=[at[0], at[1], pv], writes=[at[2]]))
                            A_(lambda: keep_f32(c0, at[2]))
                        else:
                            for (r0_, r1_, dd) in dst:
                                A_(lambda dd=dd, r0_=r0_, r1_=r1_: P.op("dve", lambda e: e.scalar_tensor_tensor(dd, at[0][r0_:r1_, :], gsc[r0_:r1_, :], at[1][r0_:r1_, :], ALU.mult, ALU.mult),
                                                                          reads=[at[0], at[1], pv], writes=[dkey]))
                    else:
                        A_(lambda: P.op("dve", lambda e: e.scalar_tensor_tensor(at[2][:], at[0][:], gsc, at[1][:], ALU.mult, ALU.mult),
                                        reads=[at[0], at[1], pv], writes=[at[2]]))
                        A_(lambda: P.op("dve", lambda e: e.tensor_copy(ab[1][:], at[2][:]), reads=[at[2]], writes=[ab[1]]))

                        def s8():
                            bk["pr"] = ring()
                            P.op("pe", lambda e: e.matmul(bk["pr"][:, 0:SUB], rm_b, ab[1][:], start=True, stop=True), reads=[ab[1], cb], writes=[bk["pr"]])
                        A_(s8)
                        A_(lambda: P.op("dve", lambda e: e.tensor_tensor(at[3][:], at[2][:], rope[:, c0:c0 + SUB], ALU.mult), reads=[at[2], rope], writes=[at[3]]))
                        A_(lambda: P.op("dve", lambda e: e.tensor_tensor(at[2][:], bk["pr"][:, 0:SUB], rope[:, LS + c0:LS + c0 + SUB], ALU.mult),
                                        reads=[bk["pr"], rope, at[2]], writes=[at[2]]))
                        for (r0_, r1_, dd) in dst:
                            A_(lambda dd=dd, r0_=r0_, r1_=r1_: P.op("dve", lambda e: e.tensor_tensor(dd, at[3][r0_:r1_, :], at[2][r0_:r1_, :], ALU.add), reads=[at[2], at[3]], writes=[dkey]))
                    return ops

                c0s = list(range(0, NT, SUB))
                for i in range(0, len(c0s), 2):
                    lists = [chain(c0s[i + t], ats[t], abs_[t]) for t in range(2) if i + t < len(c0s)]
                    for k_ in range(max(len(x) for x in lists)):
                        for lst in lists:
                            if k_ < len(lst):
                                lst[k_]()

            for qc in range(4):
                normrope(1792 + qc * 128, "qg", lambda c0, qc=qc: ([(0, 128, qfin[:, qc, c0:c0 + SUB])], (qfin, qc, c0)))
            for g in range(2):
                normrope(INW + g * 128, "kg", lambda c0, g=g: ([(0, 64, kT[0:64, g, 0, KOFF + c0:KOFF + c0 + SUB]),
                                                                (64, 128, kT[64:128, g, 1, KOFF + c0:KOFF + c0 + SUB])], (kT, g, c0)))
            if not is_s:
                def keep(c0, src):
                    for bb in range(SUB // 128):
                        pt = ring()
                        P.op("pe", lambda e, pt=pt, bb=bb, src=src: e.transpose(pt[:, 0:128], src[:, bb * 128:(bb + 1) * 128], ident), reads=[src, cf], writes=[pt])
                        vv = vo[bb % 2]
                        P.op("act", lambda e, pt=pt, vv=vv: e.copy(vv[:], pt[:, 0:128]), reads=[pt], writes=[vv])
                        tok = c0 + bb * 128
                        P.dma("sp", nk_d[tok // LP, l, tok % LP:tok % LP + 128, :], vv[:], reads=[vv], final=True)
                normrope(2304, "kg", lambda c0: (None, None), keep_f32=keep)
            s_, sv = load_w(l, 2432)
            for b in range(NT // 128):
                pvv = ring()
                ztok(l, s_, sv, b * 128, pvv)
                kb = KOFF // 128 + b
                for g in range(2):
                    P.op("act", lambda e, pvv=pvv, kb=kb, g=g: e.copy(vaug[:, kb, g, 0, 0:64], pvv[:, g * 64:(g + 1) * 64]), reads=[pvv, vaug], writes=[(vaug, kb, g)])
                    P.op("dve", lambda e, pvv=pvv, kb=kb, g=g: e.tensor_copy(vaug[:, kb, g, 1, 64:128], pvv[:, g * 64:(g + 1) * 64]), reads=[pvv, vaug], writes=[(vaug, kb, g)])
                if not is_s:
                    vv = vo[b % 2]
                    P.op("act", lambda e, pvv=pvv, vv=vv: e.copy(vv[:], pvv[:, 0:128]), reads=[pvv], writes=[vv])
                    tok = b * 128
                    P.dma("sp", nv_d[tok // LP, l, tok % LP:tok % LP + 128, :], vv[:], reads=[vv], final=True)
            items = []
            for sq in range(nseq):
                kb0 = 0 if is_s else sq * (L // 128)
                nkb = NKB if is_s else L // 128
                for o0 in range(0, L, SUB):
                    c0 = sq * L + o0
                    for h in range(8):
                        for i in range(nkb):
                            items.append((c0, h, kb0 + i, i, nkb))
            DEPTH = 2
            NPT = 5
            pTs2 = pTs + [P.sb(f"pTx{i}", [128, SUB], BF16) for i in range(NPT - len(pTs))]
            pos = {}
            pTof = {}
            for j in range(len(items) + DEPTH):
                if j < len(items):
                    c0, h, kb, i, nkb = items[j]
                    g, qc, h2 = h // 4, h // 2, h % 2
                    if i == 0:
                        pos[(c0, h)] = accb()
                    pss = ring()
                    kdeps = [(kT, g, "c")] if (is_s and kb < 4) else [(kT, g, ((kb * 128 - KOFF) // SUB) * SUB)]
                    P.op("pe", lambda e, pss=pss, h2=h2, g=g, kb=kb, qc=qc, c0=c0: e.matmul(
                        pss[:, 0:SUB], kT[:, g, h2, kb * 128:(kb + 1) * 128], qfin[:, qc, c0:c0 + SUB], start=True, stop=True),
                        reads=kdeps + [kT, (qfin, qc, c0)], writes=[pss])
                    pT = pTs2[j % NPT]
                    pTof[j] = pT
                    P.op("act", lambda e, pss=pss, pT=pT: e.activation(pT[:], pss[:, 0:SUB], AF.Exp, scale=0.125), reads=[pss], writes=[pT])
                jj = j - DEPTH
                if jj >= 0:
                    c0, h, kb, i, nkb = items[jj]
                    g, qc, h2 = h // 4, h // 2, h % 2
                    hs = slice(64 * h2, 64 * h2 + 64)
                    ds = slice(64 * (1 - h2), 64 * (1 - h2) + 64)
                    po = pos[(c0, h)]
                    pT = pTof.pop(jj)
                    vdeps = [(vaug, "c", g), vaug] if (is_s and kb < 4) else [(vaug, kb, g), vaug]
                    P.op("pe", lambda e, po=po, kb=kb, g=g, h2=h2, pT=pT, i=i, nkb=nkb: e.matmul(
                        po[:, 0:SUB], vaug[:, kb, g, h2, :], pT[:], start=(i == 0), stop=(i == nkb - 1)),
                        reads=vdeps + [pT], writes=[po])
                    if i == nkb - 1:
                        P.op("dve", lambda e, po=po, hs=hs, ds=ds: e.reciprocal(rcb[hs, :], po[ds, 0:SUB]), reads=[po], writes=[rcb])
                        P.op("dve", lambda e, po=po, hs=hs, qc=qc, c0=c0: e.tensor_tensor(ocat[hs, 4 + qc, c0:c0 + SUB], po[hs, 0:SUB], rcb[hs, :], ALU.mult),
                             reads=[po, rcb], writes=[(ocat, 4 + qc)])
            P.pop()

        lim = _LIMIT.get("lim", 99)
        if lim >= 2:
            for l in range(2):
                ffn(l, 0, f1i_d, f1o_d, False)
                if l == 0 and not _STATE.get("mod1"):
                    _STATE["mod1"] = True
                    st["modgen"] = compute_mod_gen(1)
                if lim == 2:
                    step_mod(drain=True)
                    break
                mixer(l)
                if lim <= 6:
                    break
                ffn(l, 2, f2i_d, f2o_d, l == 1)
        P.pop()

    if _LIMIT.get("lim", 99) >= 1:
        run_group(_LIMIT.get("g0", 0))
    if _LIMIT.get("lim", 99) >= 8:
        run_group(1)
    P.finish()
    return nc, P


def _consts():
    c = np.zeros((128, NCONST), np.float32)
    i = np.arange(128)
    c[:, C_ID:C_ID + 128] = np.eye(128, dtype=np.float32)
    c[:, C_ONE:C_ONE + 128] = 1.0
    c[:, C_BONE:C_BONE + 128] = (i[:, None] // 64 == i[None, :] // 64)
    same = (i[:, None] // 16 == i[None, :] // 16)
    c[:, C_MFW:C_MFW + 128] = same & (i[:, None] <= i[None, :])
    c[:, C_MBW:C_MBW + 128] = same & (i[:, None] >= i[None, :])
    d = i % 64
    ii = d % 32
    partner = np.where(ii < 16, i + 16, i - 16)
    rm = np.zeros((128, 128), np.float32)
    rm[partner, i] = 1.0
    c[:, C_RM:C_RM + 128] = rm
    c[:, C_CM:C_CM + 8] = (i[:, None] // 16 == np.arange(8)[None, :])
    seg = np.ones(512, np.float32)
    seg[::16] = 0.0
    c[:, C_SEG:C_SEG + 512] = seg[None, :]
    return c


def _rope():
    t = np.arange(LS)
    row = (t // 64).astype(np.float32)
    col = (t % 64).astype(np.float32)
    inv = (10000.0 ** (-np.arange(16, dtype=np.float32) / 16)).astype(np.float32)
    out = np.zeros((128, 2 * LS), np.float32)
    for p in range(128):
        d = p % 64
        pos = row if d < 32 else col
        ii = d % 32
        ang = (pos * inv[ii % 16]).astype(np.float32)
        out[p, :LS] = np.cos(ang)
        out[p, LS:] = -np.sin(ang) if ii < 16 else np.sin(ang)
    return out


_CACHE = {}
_LIMIT = {}


def kernel(x_prompt, x_sample, cache_k, cache_v, state_hgrn, c, c_ctx, w_mod, b_mod, ln_g, ln_b,
           ffn1_w_in, ffn1_w_out, ffn2_w_in, ffn2_w_out, w_in, w_out, hgrn_lb_logits,
           hgrn_norm_g, conv_w, conv_b, conv_ln_g, conv_ln_b, q_norm_g, k_norm_g):
    f = lambda a: np.ascontiguousarray(np.asarray(a, dtype=np.float32))
    x_prompt, x_sample, cache_k, cache_v, state_hgrn = map(f, (x_prompt, x_sample, cache_k, cache_v, state_hgrn))
    w_in = f(w_in)
    kcols = w_in[:, :, 2304:2432]
    w_in_ext = np.concatenate([w_in, kcols[:, :, 0:64], kcols[:, :, 0:64], kcols[:, :, 64:128], kcols[:, :, 64:128]], axis=2)
    consts = _consts()
    rope = _rope()

    def fm(v, nch):
        return np.asarray(v, np.float32).reshape(nch, 128).T

    shared = {
        "consts": consts, "rope": rope, "w_mod": f(w_mod), "ffn1_w_in": f(ffn1_w_in), "ffn1_w_out": f(ffn1_w_out),
        "ffn2_w_in": f(ffn2_w_in), "ffn2_w_out": f(ffn2_w_out), "w_in_ext": np.ascontiguousarray(w_in_ext), "w_out": f(w_out),
    }
    in_maps = []
    for core in range(8):
        b = core // 4
        pvv = np.zeros((128, NPV), np.float32)
        for l in range(2):
            pvv[:, PV["bmod"] + l * 72: PV["bmod"] + (l + 1) * 72] = fm(np.asarray(b_mod)[l], 72)
            for s3 in range(3):
                o = (l * 3 + s3) * 8
                pvv[:, PV["lng"] + o: PV["lng"] + o + 8] = fm(np.asarray(ln_g)[l, s3], 8)
                pvv[:, PV["lnb"] + o: PV["lnb"] + o + 8] = fm(np.asarray(ln_b)[l, s3], 8)
            pvv[:, PV["lbl"] + l * 2: PV["lbl"] + l * 2 + 2] = fm(np.asarray(hgrn_lb_logits)[l], 2)
            pvv[:, PV["hg"] + l * 2: PV["hg"] + l * 2 + 2] = fm(np.asarray(hgrn_norm_g)[l], 2)
            for cc in range(2):
                o = (l * 2 + cc) * 31
                pvv[:, PV["cw"] + o: PV["cw"] + o + 31] = np.asarray(conv_w, np.float32)[l][:, cc * 128:(cc + 1) * 128].T
            pvv[:, PV["cb"] + l * 2: PV["cb"] + l * 2 + 2] = fm(np.asarray(conv_b)[l], 2)
            pvv[:, PV["clg"] + l * 2: PV["clg"] + l * 2 + 2] = fm(np.asarray(conv_ln_g)[l], 2)
            pvv[:, PV["clb"] + l * 2: PV["clb"] + l * 2 + 2] = fm(np.asarray(conv_ln_b)[l], 2)
            pvv[:, PV["qg"] + l] = np.tile(np.asarray(q_norm_g, np.float32)[l], 2)
            pvv[:, PV["kg"] + l] = np.tile(np.asarray(k_norm_g, np.float32)[l], 2)
        cond = np.stack([fm(c_ctx, 8), fm(np.asarray(c)[b], 8)], axis=2)
        pvv[:, PV["cond"]: PV["cond"] + 16] = cond.reshape(128, 16)
        ck = cache_k[b]
        ckT = np.transpose(ck, (0, 2, 3, 1))
        ckT = np.concatenate([ckT, ckT], axis=2)
        m = dict(shared)
        m.update({
            "xp": x_prompt[core * 4:(core + 1) * 4].reshape(NSEQ * LP, D),
            "xs": x_sample[b],
            "ckT": np.ascontiguousarray(ckT),
            "cv": np.ascontiguousarray(cache_v[b].reshape(2, PAST, 128)),
            "st0": np.ascontiguousarray(state_hgrn[b].reshape(2, 2, 2, 128, 64)),
            "pv": pvv,
        })
        in_maps.append(m)
    if _LIMIT.get("prep_only"):
        return in_maps
    if "nc" not in _CACHE:
        _CACHE["nc"] = build_program()[0]
    res = run_bass_kernel_spmd(_CACHE["nc"], in_maps, core_ids=list(range(8)))
    r = res.results
    y_prompt = np.concatenate([r[i]["yp"].reshape(NSEQ, LP, D) for i in range(8)], axis=0)
    y_sample = np.stack([r[0]["ys"], r[4]["ys"]], axis=0)
    nk = np.concatenate([r[i]["nk"].reshape(NSEQ, 2, LP, 2, 64) for i in range(8)], axis=0)
    nv = np.concatenate([r[i]["nv"].reshape(NSEQ, 2, LP, 2, 64) for i in range(8)], axis=0)
    nst = np.concatenate([r[i]["nst"].reshape(NSEQ, 2, 2, 4, 64, 64) for i in range(8)], axis=0)
    return (y_prompt.astype(np.float32), y_sample.astype(np.float32), nk.astype(np.float32),
            nv.astype(np.float32), nst.astype(np.float32))
```
